# Optimizing a Trainium2 kernel written in Bass

```python
import jax
import jax.numpy as jnp
from jax import lax
import numpy as np

D_MODEL = 2048
BATCH = 4
SEQ = 8192
DEPTH = 2
DEC_BATCH = 8
DEC_SEQ = 2048
PAST_LEN = 128

MIX_WIDTH = D_MODEL
HG_WIDTH = MIX_WIDTH // 2
POOL_WIDTH = MIX_WIDTH - HG_WIDTH
HG_HEAD_K = 128
HG_HEADS = HG_WIDTH // HG_HEAD_K
HG_HEAD_V = HG_WIDTH // HG_HEADS
POOL_WINDOWS = (2, 4, 8, 16)
POOL_GROUPS = len(POOL_WINDOWS)
POOL_GROUP_WIDTH = POOL_WIDTH // POOL_GROUPS
HG_COLS = 5 * HG_WIDTH
IN_COLS = HG_COLS + POOL_WIDTH
D_FF = 4 * D_MODEL
N_META = 16
CHUNK = 64
META_PAD = (-N_META) % CHUNK
EPS = 1e-6
FORGET_FLOOR = 1e-30

kernel_name = 'hymba_hgrn2_pool_bidir_encoder'


def rms_norm(x, gain):
    xf = x.astype(jnp.float32)
    xf = xf * lax.rsqrt(jnp.mean(xf * xf, axis=-1, keepdims=True) + EPS)
    return (xf * gain.astype(jnp.float32)).astype(x.dtype)


def gla_chunk_scan(q, k, v, log_f):
    B, L, H, DK = q.shape
    DV = v.shape[-1]
    n_chunks = L // CHUNK

    def to_chunks(a):
        return a.reshape(B, n_chunks, CHUNK, H, a.shape[-1]).transpose(1, 0, 3, 2, 4)

    inclusive = jnp.tril(jnp.ones((CHUNK, CHUNK), dtype=bool))[:, :, None]

    def step(state, xs):
        qc, kc, vc, gc = xs
        b = jnp.cumsum(gc, axis=2)
        rel = jnp.where(inclusive, b[:, :, :, None, :] - b[:, :, None, :, :], 0.0)
        decay = jnp.where(inclusive, jnp.exp(rel), 0.0)
        scores = jnp.einsum('bhtd,bhsd,bhtsd->bhts', qc, kc, decay)
        out = (jnp.einsum('bhts,bhsv->bhtv', scores, vc)
               + jnp.einsum('bhtd,bhdv->bhtv', qc * jnp.exp(b), state))
        b_last = b[:, :, -1:, :]
        state = (state * jnp.exp(b_last[:, :, 0, :, None])
                 + jnp.einsum('bhsd,bhsv->bhdv', kc * jnp.exp(b_last - b), vc))
        return state, out

    state0 = jnp.zeros((B, H, DK, DV), jnp.float32)
    _, out = lax.scan(step, state0, (to_chunks(q), to_chunks(k), to_chunks(v), to_chunks(log_f)))
    return out.transpose(1, 0, 3, 2, 4).reshape(B, L, H, DV)


def hgrn2_mixer(u, lb_fwd, lb_bwd, head_norm):
    B, L, _ = u.shape
    uf = u.astype(jnp.float32)
    q, f_fwd, f_bwd, inp, gate = jnp.split(uf, 5, axis=-1)
    q = jax.nn.silu(q)

    def forget(f_pre, lb):
        sig = jax.nn.sigmoid(f_pre)
        f = lb + (1.0 - lb) * sig
        log_f = jnp.log(jnp.maximum(f, FORGET_FLOOR))
        k = (1.0 - lb) * (1.0 - sig)
        return log_f, k

    lf_fwd, k_fwd = forget(f_fwd, lb_fwd)
    lf_bwd, k_bwd = forget(f_bwd, lb_bwd)

    def heads_padded(a):
        a = a.reshape(B, L, HG_HEADS, -1)
        return jnp.pad(a, ((0, 0), (META_PAD, 0), (0, 0), (0, 0)))

    q, inp, lf_fwd, k_fwd, lf_bwd, k_bwd = map(heads_padded, (q, inp, lf_fwd, k_fwd, lf_bwd, k_bwd))
    rev = lambda a: jnp.flip(a, axis=1)
    o = (gla_chunk_scan(q, k_fwd, inp, lf_fwd)
         + rev(gla_chunk_scan(rev(q), rev(k_bwd), rev(inp), rev(lf_bwd))))
    o = o[:, META_PAD:]
    o = (o * lax.rsqrt(jnp.mean(o * o, axis=-1, keepdims=True) + EPS)
         * head_norm.astype(jnp.float32).reshape(HG_HEADS, HG_HEAD_V))
    return o.reshape(B, L, HG_WIDTH) * jax.nn.silu(gate)


def pool_mixer(u, w_pool, pool_scale):
    B, L, _ = u.shape
    uf = u.astype(jnp.float32).reshape(B, L, POOL_GROUPS, POOL_GROUP_WIDTH)
    csum = jnp.pad(jnp.cumsum(uf, axis=1), ((0, 0), (1, 0), (0, 0), (0, 0)))
    pos = jnp.arange(L)
    pooled = []
    for g, window in enumerate(POOL_WINDOWS):
        lo = jnp.clip(pos - window // 2, 0, L)
        hi = jnp.clip(pos + window - window // 2, 0, L)
        total = jnp.take(csum[:, :, g], hi, axis=1) - jnp.take(csum[:, :, g], lo, axis=1)
        mean = total / (hi - lo).astype(jnp.float32)[None, :, None]
        pooled.append(mean - uf[:, :, g])
    pooled = jnp.stack(pooled, axis=2)
    y = jnp.einsum('blgc,gcd->blgd', pooled, w_pool.astype(jnp.float32))
    return y.reshape(B, L, POOL_WIDTH) * pool_scale.astype(jnp.float32)


def encoder_layer(h, w_in, w_pool, pool_scale, lb_fwd, lb_bwd, head_norm, w_out,
                  norm_mix, norm_mlp, w_up, w_down):
    a = rms_norm(h, norm_mix)
    u = jnp.einsum('bld,dc->blc', a, w_in.astype(h.dtype))
    y_hg = hgrn2_mixer(u[..., :HG_COLS], lb_fwd, lb_bwd, head_norm)
    y_pool = pool_mixer(u[..., HG_COLS:], w_pool, pool_scale)
    mixed = jnp.concatenate([y_hg, y_pool], axis=-1).astype(h.dtype)
    h = h + jnp.einsum('blc,cd->bld', mixed, w_out.astype(h.dtype))
    m = rms_norm(h, norm_mlp)
    hidden = jnp.square(jax.nn.relu(jnp.einsum('bld,df->blf', m, w_up.astype(h.dtype))))
    return h + jnp.einsum('blf,fd->bld', hidden, w_down.astype(h.dtype))


def encoder_trunk(x, meta_tokens, lower, w_in, w_pool, pool_scale, hg_head_norm, w_out,
                  norm_mix, norm_mlp, w_up, w_down, final_norm):
    B = x.shape[0]
    meta = jnp.broadcast_to(meta_tokens.astype(x.dtype)[None], (B, N_META, D_MODEL))
    h = jnp.concatenate([meta, x], axis=1)
    for l in range(DEPTH):
        h = encoder_layer(h, w_in[l], w_pool[l], pool_scale[l], lower[0, l], lower[1, l],
                          hg_head_norm[l], w_out[l], norm_mix[l], norm_mlp[l], w_up[l], w_down[l])
    return rms_norm(h, final_norm)[:, N_META:]


def setup_inputs(seed: int = 0) -> dict:
    key = jax.random.key(seed)
    ks = jax.random.split(key, 14)
    f32 = jnp.float32

    def nrm(k, shape, scale):
        return jax.random.normal(k, shape, f32) * scale

    return {
        'x_prompt': nrm(ks[0], (BATCH, SEQ, D_MODEL), 1.0),
        'x_sample': nrm(ks[1], (DEC_BATCH, DEC_SEQ, D_MODEL), 1.0),
        'meta_tokens': nrm(ks[2], (N_META, D_MODEL), 1.0),
        'w_in': nrm(ks[3], (DEPTH, D_MODEL, IN_COLS), D_MODEL ** -0.5),
        'w_pool': nrm(ks[4], (DEPTH, POOL_GROUPS, POOL_GROUP_WIDTH, POOL_GROUP_WIDTH), POOL_GROUP_WIDTH ** -0.5),
        'pool_scale': 1.0 + nrm(ks[5], (DEPTH, POOL_WIDTH), 0.1),
        'hg_lower_bound': nrm(ks[6], (2, DEPTH, HG_WIDTH), 1.0),
        'hg_head_norm': 1.0 + nrm(ks[7], (DEPTH, HG_WIDTH), 0.1),
        'w_out': nrm(ks[8], (DEPTH, MIX_WIDTH, D_MODEL), MIX_WIDTH ** -0.5),
        'norm_mix': 1.0 + nrm(ks[9], (DEPTH, D_MODEL), 0.1),
        'norm_mlp': 1.0 + nrm(ks[10], (DEPTH, D_MODEL), 0.1),
        'w_up': nrm(ks[11], (DEPTH, D_MODEL, D_FF), D_MODEL ** -0.5),
        'w_down': nrm(ks[12], (DEPTH, D_FF, D_MODEL), D_FF ** -0.5),
        'final_norm': 1.0 + nrm(ks[13], (D_MODEL,), 0.1),
    }


def reference(x_prompt, x_sample, meta_tokens, w_in, w_pool, pool_scale, hg_lower_bound,
              hg_head_norm, w_out, norm_mix, norm_mlp, w_up, w_down, final_norm):
    probs = jax.nn.softmax(hg_lower_bound.astype(jnp.float32), axis=1)
    lower = jnp.cumsum(probs, axis=1) - probs[:, :1]
    y_prompt = encoder_trunk(x_prompt, meta_tokens, lower, w_in, w_pool, pool_scale, hg_head_norm,
                             w_out, norm_mix, norm_mlp, w_up, w_down, final_norm)
    y_sample = encoder_trunk(x_sample, meta_tokens, lower, w_in, w_pool, pool_scale, hg_head_norm,
                             w_out, norm_mix, norm_mlp, w_up, w_down, final_norm)
    return (y_prompt, y_sample)
```

```python
import numpy as np
from contextlib import ExitStack
import concourse.bass as bass
import concourse.mybir as mybir
from concourse.bass_utils import run_bass_kernel_spmd

F32 = mybir.dt.float32
BF16 = mybir.dt.bfloat16
U8 = mybir.dt.uint8
AF = mybir.ActivationFunctionType
ALU = mybir.AluOpType

D = 2048
KC = 16
NH = 8
DFF = 8192
EPS = 1e-6
NMETA = 16
HEAD = 64
TT = 256
POOLW = (2, 4, 8, 16)
ENG = ("pe", "act", "dve", "pool", "sp")
SAME_SYNC = True


class Ins:
    __slots__ = ("eng", "fn", "dkey", "signal", "deps", "sem", "val", "inc", "waits", "nonc")


class Rec:
    def __init__(self):
        self.call = None

    def __getattr__(self, name):
        def f(*a, **k):
            self.call = (name, a, k)
            return self
        return f


class Prog:
    def __init__(self):
        self.q = {e: [] for e in ENG}
        self.all = []
        self.lastw = {}
        self.readers = {}
        self.last_dma = {}

    def op(self, eng, fn, r=(), w=(), dkey=None, extra=()):
        ins = Ins()
        ins.nonc = False
        if fn is not None:
            rec = Rec()
            fn(rec)
            fn = rec.call
        ins.eng, ins.fn, ins.dkey = eng, fn, dkey
        ins.signal = dkey is not None
        deps = {}
        for k in r:
            x = self.lastw.get(k)
            if x is not None:
                deps[id(x)] = x
        for k in w:
            x = self.lastw.get(k)
            if x is not None:
                deps[id(x)] = x
            for y in self.readers.get(k, {}).values():
                deps[id(y)] = y
        for x in extra:
            deps[id(x)] = x
        for k in r:
            self.readers.setdefault(k, {})[eng if dkey is None else ("dma", dkey, len(self.all))] = ins
        for k in w:
            self.lastw[k] = ins
            self.readers[k] = {}
        ins.deps = []
        for d in deps.values():
            if d is ins:
                continue
            if d.dkey is None and d.eng == eng and not (SAME_SYNC and eng in ("act", "dve", "pool")):
                continue
            d.signal = True
            ins.deps.append(d)
        self.q[eng].append(ins)
        self.all.append(ins)
        if dkey is not None:
            self.last_dma[dkey] = ins
        return ins

    def barrier(self):
        lasts = [self.q[e][-1] for e in ENG if self.q[e]] + list(self.last_dma.values())
        for e in ENG:
            self.op(e, None, extra=lasts)

    def finalize(self, nc, stack):
        sems = {e: stack.enter_context(nc.semaphore("s_" + e)) for e in ENG}
        cnt = {e: 0 for e in ENG}
        dcnt = {}
        for ins in self.all:
            if ins.dkey is not None:
                if ins.dkey not in sems:
                    sems[ins.dkey] = stack.enter_context(nc.semaphore("d%d" % len(sems)))
                    dcnt[ins.dkey] = 0
                dcnt[ins.dkey] += 16
                ins.sem, ins.val, ins.inc = sems[ins.dkey], dcnt[ins.dkey], 16
            elif ins.signal and ins.fn is not None:
                cnt[ins.eng] += 1
                ins.sem, ins.val, ins.inc = sems[ins.eng], cnt[ins.eng], 1
            elif ins.signal:
                ins.sem = None
        for e in ENG:
            waited = {}
            for ins in self.q[e]:
                ws = []
                stack_ = list(ins.deps)
                while stack_:
                    d = stack_.pop()
                    if d.fn is None:
                        stack_.extend(d.deps)
                        continue
                    key = id(d.sem)
                    if waited.get(key, 0) < d.val:
                        waited[key] = d.val
                        ws.append((d.sem, d.val))
                best = {}
                for s, v in ws:
                    if id(s) not in best or best[id(s)][1] < v:
                        best[id(s)] = (s, v)
                ins.waits = list(best.values())
        self.nsem = len(sems)

    def replay(self, nc, block):
        engs = {"pe": block.tensor, "act": block.scalar, "dve": block.vector, "pool": block.gpsimd, "sp": block.sync}
        for e in ENG:
            lst = self.q[e]

            def body(eng, lst=lst):
                for ins in lst:
                    for s, v in ins.waits:
                        eng.wait_ge(s, v)
                    if ins.fn is None:
                        continue
                    name, a, k = ins.fn
                    if ins.nonc:
                        with nc.allow_non_contiguous_dma(reason="small strided transfer"):
                            r = getattr(eng, name)(*a, **k)
                    else:
                        r = getattr(eng, name)(*a, **k)
                    if ins.signal:
                        r.then_inc(ins.sem, ins.inc)
            engs[e](body)


class Arena:
    def __init__(self, t, size):
        self.t, self.size, self.off = t, size, 0

    def mark(self):
        return self.off

    def reset(self, m):
        self.off = m

    def alloc(self, shape, dt):
        n = 1
        for s in shape[1:]:
            n *= s
        nb = n * (4 if dt == F32 else 2)
        nb = (nb + 63) // 64 * 64
        assert self.off + nb <= self.size, ("arena overflow", self.off, nb, self.size)
        v = self.t[:, self.off:self.off + nb].bitcast(dt)
        v = v[:, 0:n]
        self.off += nb
        if len(shape) == 3:
            v = v.rearrange("p (a b) -> p a b", a=shape[1])
        elif len(shape) == 4:
            v = v.rearrange("p (a b c) -> p a b c", a=shape[1], b=shape[2])
        return v


DEBUG = {}


def build_program(SEGS, debug_layers=2, debug=False):
    nc = bass.Bass("TRN2", target_bir_lowering=False)
    NS = len(SEGS)
    LT = [HEAD + s for s in SEGS]
    NCH = [l // 64 for l in LT]
    P = Prog()

    def dram(name, shape, dt, kind=None):
        if kind:
            return nc.dram_tensor(name, shape, dt, kind=kind).ap()
        return nc.dram_tensor(name, shape, dt).ap()

    x_d = [dram("x%d" % s, [SEGS[s], D], F32, "ExternalInput") for s in range(NS)]
    y_d = [dram("y%d" % s, [SEGS[s], D], F32, "ExternalOutput") for s in range(NS)]
    meta_d = dram("meta_tokens", [NMETA, D], F32, "ExternalInput")
    w_in_d = dram("w_in", [2, D, 6144], F32, "ExternalInput")
    w_pool_d = dram("w_pool", [2, 4, 256, 256], F32, "ExternalInput")
    pscale_d = dram("pool_scale", [2, 1024], F32, "ExternalInput")
    lbraw_d = dram("hg_lower_bound", [2, 2, 1024], F32, "ExternalInput")
    hn_d = dram("hg_head_norm", [2, 1024], F32, "ExternalInput")
    w_out_d = dram("w_out", [2, D, D], F32, "ExternalInput")
    nmix_d = dram("norm_mix", [2, D], F32, "ExternalInput")
    nmlp_d = dram("norm_mlp", [2, D], F32, "ExternalInput")
    w_up_d = dram("w_up", [2, D, DFF], F32, "ExternalInput")
    w_dn_d = dram("w_down", [2, DFF, D], F32, "ExternalInput")
    fnorm_d = dram("final_norm", [D], F32, "ExternalInput")

    H1 = [dram("H1_%d" % s, [LT[s], D], F32) for s in range(NS)]
    OP = [dram("OP_%d" % s, [NH, 128, LT[s]], F32) for s in range(NS)]
    QB = [dram("QB_%d" % s, [NH, 128, LT[s]], BF16) for s in range(NS)]
    GT = [dram("GT_%d" % s, [NH, 128, LT[s]], BF16) for s in range(NS)]
    KBT = [dram("KBT_%d" % s, [LT[s], 1024], BF16) for s in range(NS)]
    VT = [dram("VT_%d" % s, [LT[s], 1024], BF16) for s in range(NS)]
    UP = [dram("UP_%d" % s, [NH, 128, LT[s] + 16], F32) for s in range(NS)]
    WIN = [dram("WIN_%d" % l, [12, 128, KC, 512], BF16) for l in range(2)]
    WOUT = [dram("WOUT_%d" % l, [4, 128, KC, 512], BF16) for l in range(2)]
    WUP = [dram("WUP_%d" % l, [16, 128, KC, 512], BF16) for l in range(2)]
    WDN = [dram("WDN_%d" % l, [16, 128, KC, 512], BF16) for l in range(2)]

    tiles = [dict(T=128, nb=1, head=True, parts=[(s, 0, 64, 64 * s) for s in range(NS)],
                  chunks=[(s, 0) for s in range(NS)], last=False)]
    assert NS == 2
    for s in range(NS):
        nt = SEGS[s] // TT
        for i in range(nt):
            tiles.append(dict(T=TT, nb=TT // 128, head=False, parts=[(s, HEAD + i * TT, TT, 0)],
                              chunks=[(s, 1 + (TT // 64) * i + j) for j in range(TT // 64)],
                              last=(i == nt - 1), seg=s))

    stack = ExitStack()
    with stack:
        ARENA_BYTES = 188 * 1024
        arena_t = stack.enter_context(nc.sbuf_tensor("arena", [128, ARENA_BYTES], U8))
        A = Arena(arena_t, ARENA_BYTES)
        psmm = [stack.enter_context(nc.psum_tensor("psmm%d" % i, [128, 512], F32)) for i in range(4)]
        psT = stack.enter_context(nc.psum_tensor("psT", [128, 1024], F32))
        psX = stack.enter_context(nc.psum_tensor("psX", [128, 1024], F32))

        ident = A.alloc([128, 128], BF16)
        ones = A.alloc([128, 128], BF16)
        maskU = A.alloc([128, 64], F32)
        maskL = A.alloc([128, 64], F32)
        scanm = A.alloc([128, TT], F32)
        epsc = A.alloc([128, 2], F32)
        lbr = A.alloc([128, 2, 2, NH], F32)
        lbt = A.alloc([128, 8, NH], F32)
        lb = A.alloc([128, 2, 2, NH], F32)
        oml = A.alloc([128, 2, 2, NH], F32)
        noml = A.alloc([128, 2, 2, NH], F32)
        gmix = A.alloc([128, 2, KC], F32)
        gmlp = A.alloc([128, 2, KC], F32)
        hng = A.alloc([128, 2, NH], F32)
        psc = A.alloc([128, 2, NH], F32)
        fgain = A.alloc([128, D], F32)
        wpool = A.alloc([128, 2, 4, 2, 256], BF16) if False else A.alloc([128, 2 * 4 * 2 * 256], BF16)
        wpool = wpool.rearrange("p (l g k c) -> p l g k c", l=2, g=4, k=2)
        zeros = A.alloc([128, 16], F32)
        Sst = [A.alloc([128, NH, 128], F32) for s in range(NS)]
        Sbf = [A.alloc([128, NH, 128], BF16) for s in range(NS)]
        expG = [A.alloc([128, NCH[s], NH], F32) for s in range(NS)]
        wbuf = [A.alloc([128, KC, 512], BF16) for i in range(2)]
        base_mark = A.mark()

        def dbg(name, ap, keys, shape, dt):
            if not debug or name in DEBUG:
                return
            d = nc.dram_tensor("dbg_" + name, list(shape), dt, kind="ExternalOutput").ap()
            DEBUG[name] = d
            dma("sp", d, ap, keys, [], ("dbg", name))

        def memset(eng, ap, val, w):
            return P.op(eng, lambda e, ap=ap, val=val: e.memset(ap, val), w=w)

        def dma(eng, out, in_, r, w, key, nonc=False):
            ins = P.op(eng, lambda e: e.dma_start(out=out, in_=in_), r=r, w=w, dkey=key)
            ins.nonc = nonc
            return ins

        memset("dve", ident, 0.0, ["ident"])
        P.op("pool", lambda e: e.affine_select(out=ident, in_=ident, pattern=[[-1, 128]], compare_op=ALU.not_equal,
                                               fill=1.0, base=0, channel_multiplier=1), r=["ident"], w=["ident"])
        memset("dve", ones, 1.0, ["ones"])
        memset("dve", maskU, 1.0, ["maskU"])
        memset("dve", maskL, 1.0, ["maskL"])
        for half in range(2):
            sl = slice(64 * half, 64 * half + 64)
            P.op("pool", lambda e, sl=sl, half=half: e.affine_select(
                out=maskU[sl, :], in_=maskU[sl, :], pattern=[[1, 64]], compare_op=ALU.is_ge, fill=0.0,
                base=0, channel_multiplier=-1), r=["maskU"], w=["maskU"])
            P.op("pool", lambda e, sl=sl, half=half: e.affine_select(
                out=maskL[sl, :], in_=maskL[sl, :], pattern=[[-1, 64]], compare_op=ALU.is_ge, fill=0.0,
                base=0, channel_multiplier=1), r=["maskL"], w=["maskL"])
        memset("dve", scanm, 1.0, ["scanm"])
        for c in range(TT // 64):
            memset("dve", scanm[:, c * 64:c * 64 + 1], 0.0, ["scanm"])
        memset("dve", epsc[:, 0:1], EPS, ["epsc"])
        memset("dve", epsc[:, 1:2], 1e-30, ["epsc"])
        memset("dve", zeros, 0.0, ["zeros"])
        dma("sp", lbr.rearrange("p a b h -> p (a b) h"), lbraw_d.rearrange("a b (h p) -> p (a b) h", p=128),
            [], ["lbr"], "par_lbr", nonc=True)
        dma("sp", gmix, nmix_d.rearrange("l (k p) -> p l k", p=128), [], ["gmix"], "par", nonc=True)
        dma("sp", gmlp, nmlp_d.rearrange("l (k p) -> p l k", p=128), [], ["gmlp"], "par", nonc=True)
        dma("sp", hng, hn_d.rearrange("l (h p) -> p l h", p=128), [], ["hng"], "par", nonc=True)
        dma("sp", psc, pscale_d.rearrange("l (h p) -> p l h", p=128), [], ["psc"], "par", nonc=True)
        dma("sp", fgain, fnorm_d.partition_broadcast(128), [], ["fgain"], "par")
        for l in range(2):
            for g in range(4):
                dma("pool", wpool[:, l, g, :, :], w_pool_d[l, g].rearrange("(k p) c -> p k c", p=128),
                    [], ["wpool"], "par2")
        for s in range(NS):
            dma("sp", UP[s][:, :, 0:8].rearrange("h p t -> p h t"), zeros[:, 0:8].unsqueeze(1).broadcast_to([128, NH, 8]),
                ["zeros"], [("UPpad", s)], "par", nonc=True)
            dma("sp", UP[s][:, :, LT[s] + 8:LT[s] + 16].rearrange("h p t -> p h t"),
                zeros[:, 0:8].unsqueeze(1).broadcast_to([128, NH, 8]), ["zeros"], [("UPpad", s)], "par", nonc=True)
        for a in range(2):
            x0, x1 = lbr[:, a, 0, :], lbr[:, a, 1, :]
            m, e0, e1, sm = lbt[:, 0, :], lbt[:, 1, :], lbt[:, 2, :], lbt[:, 3, :]
            P.op("dve", lambda e, x0=x0, x1=x1, m=m: e.tensor_tensor(out=m, in0=x0, in1=x1, op=ALU.max), r=["lbr"], w=["lbt"])
            P.op("dve", lambda e, x0=x0, m=m, e0=e0: e.tensor_tensor(out=e0, in0=x0, in1=m, op=ALU.subtract), r=["lbt"], w=["lbt"])
            P.op("dve", lambda e, x1=x1, m=m, e1=e1: e.tensor_tensor(out=e1, in0=x1, in1=m, op=ALU.subtract), r=["lbt"], w=["lbt"])
            P.op("act", lambda e, e0=e0: e.activation(out=e0, in_=e0, func=AF.Exp), r=["lbt"], w=["lbt"])
            P.op("act", lambda e, e1=e1: e.activation(out=e1, in_=e1, func=AF.Exp), r=["lbt"], w=["lbt"])
            P.op("dve", lambda e, e0=e0, e1=e1, sm=sm: e.tensor_tensor(out=sm, in0=e0, in1=e1, op=ALU.add), r=["lbt"], w=["lbt"])
            P.op("dve", lambda e, sm=sm: e.reciprocal(out=sm, in_=sm), r=["lbt"], w=["lbt"])
            P.op("dve", lambda e, e0=e0, sm=sm: e.tensor_tensor(out=e0, in0=e0, in1=sm, op=ALU.mult), r=["lbt"], w=["lbt"])
            P.op("dve", lambda e, e1=e1, sm=sm: e.tensor_tensor(out=e1, in0=e1, in1=sm, op=ALU.mult), r=["lbt"], w=["lbt"])
            P.op("dve", lambda e, e0=e0, a=a: e.tensor_tensor(out=lb[:, 0, a, :], in0=e0, in1=e0, op=ALU.subtract), r=["lbt"], w=["lb"])
            P.op("dve", lambda e, e0=e0, e1=e1, sm=sm: e.tensor_tensor(out=sm, in0=e0, in1=e1, op=ALU.add), r=["lbt"], w=["lbt"])
            P.op("dve", lambda e, e0=e0, sm=sm, a=a: e.tensor_tensor(out=lb[:, 1, a, :], in0=sm, in1=e0, op=ALU.subtract), r=["lbt"], w=["lb"])
        lbf, omlf, nomlf = [t.rearrange("p a b h -> p (a b h)") for t in (lb, oml, noml)]
        P.op("dve", lambda e: e.tensor_scalar(out=omlf, in0=lbf, scalar1=-1.0, scalar2=1.0, op0=ALU.mult, op1=ALU.add), r=["lb"], w=["oml"])
        P.op("dve", lambda e: e.tensor_scalar(out=nomlf, in0=lbf, scalar1=1.0, scalar2=-1.0, op0=ALU.mult, op1=ALU.add), r=["lb"], w=["oml"])
        lbe = A.alloc([128, 2, 2, NH], F32)
        base_mark = A.mark()
        P.op("dve", lambda e: e.scalar_tensor_tensor(out=lbe.rearrange("p a b h -> p (a b h)"), in0=omlf, scalar=1e-30, in1=lbf,
                                                     op0=ALU.mult, op1=ALU.add), r=["lb", "oml"], w=["lbe"])

        def conv(l):
            wi = w_in_d[l].rearrange("(k p) c -> p k c", p=128)
            for j in range(12):
                dma("pool", WIN[l][j], wi[:, :, j * 512:(j + 1) * 512], [], [("WIN", l, j)], ("cv", j % 4))
            wo = w_out_d[l].rearrange("(k p) c -> p k c", p=128)
            for j in range(4):
                dma("pool", WOUT[l][j], wo[:, :, j * 512:(j + 1) * 512], [], [("WOUT", l, j)], ("cv", j % 4))
            wu = w_up_d[l].rearrange("(k p) c -> p k c", p=128)
            for j in range(16):
                dma("pool", WUP[l][j], wu[:, :, j * 512:(j + 1) * 512], [], [("WUP", l, j)], ("cv", j % 4))
            wd = w_dn_d[l].rearrange("(g k p) c -> g p k c", p=128, k=KC)
            for fg in range(4):
                for pn in range(4):
                    dma("pool", WDN[l][fg * 4 + pn], wd[fg][:, :, pn * 512:(pn + 1) * 512], [], [("WDN", l, fg * 4 + pn)], ("cv", pn))
        conv(0)
        if debug_layers > 1:
            conv(1)
        P.barrier()

        wstate = {"i": 0}

        def load_piece(dram_piece, key):
            slot = wstate["i"] % 2
            wstate["i"] += 1
            dma("sp", wbuf[slot], dram_piece, [key], [("wbuf", slot)], ("wb", slot))
            return slot

        def mm(out, lhsT, rhs, start, stop, r, w):
            return P.op("pe", lambda e: e.matmul(out, lhsT=lhsT, rhs=rhs, start=start, stop=stop), r=r, w=w)

        def norm_transpose(tile, src, srckey, gain, aT, aTkey, ab, ss):
            for tb in range(tile["nb"]):
                slot = tb % 2
                hsrc = src(tb)
                abk = ("ab", slot)
                P.op("act", lambda e, hsrc=hsrc, slot=slot, tb=tb: e.activation(
                    out=ab[slot], in_=hsrc, func=AF.Square, accum_out=ss[:, tb:tb + 1]), r=[srckey(tb)], w=[abk, ("ss", tb)])
                P.op("act", lambda e, tb=tb: e.activation(out=ss[:, 8 + tb:9 + tb], in_=ss[:, tb:tb + 1], func=AF.Ln,
                                                          scale=1.0 / D, bias=epsc[:, 0:1]), r=[("ss", tb)], w=[("ss", tb)])
                P.op("act", lambda e, tb=tb: e.activation(out=ss[:, 16 + tb:17 + tb], in_=ss[:, 8 + tb:9 + tb], func=AF.Exp,
                                                          scale=-0.5), r=[("ss", tb)], w=[("ss", tb)])
                P.op("act", lambda e, hsrc=hsrc, slot=slot, tb=tb: e.mul(
                    out=ab[slot], in_=hsrc, mul=ss[:, 16 + tb:17 + tb]), r=[srckey(tb), ("ss", tb)], w=[abk])
                for k4 in range(4):
                    half = k4 % 2
                    pk = ("psT", half)
                    for j in range(4):
                        kc = k4 * 4 + j
                        mm(psT[:, half * 512 + j * 128: half * 512 + (j + 1) * 128], ab[slot][:, kc * 128:(kc + 1) * 128], ident[:, :],
                           True, True, [abk], [pk])
                    gb = gain[:, k4 * 4:(k4 + 1) * 4].unsqueeze(2).broadcast_to([128, 4, 128])
                    P.op("dve", lambda e, half=half, k4=k4, tb=tb, gb=gb: e.tensor_tensor(
                        out=aT[:, k4 * 4:(k4 + 1) * 4, tb * 128:(tb + 1) * 128],
                        in0=psT[:, half * 512:(half + 1) * 512].rearrange("p (a b) -> p a b", a=4), in1=gb, op=ALU.mult),
                        r=[pk], w=[aTkey])

        def proj_fm(slot, aT, aTkey, T, evac, kcs=KC):
            for cb in range(4):
                bank = cb % 4
                pk = ("psmm", bank)
                for kc in range(kcs):
                    mm(psmm[bank][:, 0:T], wbuf[slot][:, kc, cb * 128:(cb + 1) * 128], aT[:, kc, 0:T], kc == 0, kc == kcs - 1,
                       [("wbuf", slot), aTkey], [pk])
                evac(cb, psmm[bank][:, 0:T], pk)

        def proj_tm(slot, actT, actkey, nb, evac, kcs=KC):
            for tb in range(nb):
                bank = tb % 4
                pk = ("psmm", bank)
                for kc in range(kcs):
                    mm(psmm[bank][:, :], actT[:, kc, tb * 128:(tb + 1) * 128], wbuf[slot][:, kc, :], kc == 0, kc == kcs - 1,
                       [("wbuf", slot), actkey], [pk])
                evac(tb, psmm[bank][:, :], pk)

        def fm_dma(eng, dr, sb, tile, to_dram, key, rk, wk, halo=0):
            for (s, slot0, ntok, toff) in tile["parts"]:
                if halo:
                    d = dr[s][:, :, slot0:slot0 + ntok + 2 * halo].rearrange("h p t -> p h t")
                    b = sb[:, :, toff:toff + ntok + 2 * halo]
                else:
                    d = dr[s][:, :, slot0:slot0 + ntok].rearrange("h p t -> p h t")
                    b = sb[:, :, toff:toff + ntok]
                if to_dram:
                    dma(eng, d, b, rk, wk + [(key, "d", s)], (key, "s"))
                else:
                    dma(eng, b, d, rk + [(key, "d", s)], wk, (key, "l"))

        def tm_dma(eng, dr, sb, tile, to_dram, key, rk, wk, rows=None):
            for (s, slot0, ntok, toff) in tile["parts"]:
                if ntok % 128 == 0:
                    d = dr[s][slot0:slot0 + ntok, :].rearrange("(b p) c -> p b c", p=128)
                    b = sb[:, toff // 128: toff // 128 + ntok // 128, :]
                else:
                    lo, hi = (0, ntok) if rows is None else rows
                    d = dr[s][slot0 + lo:slot0 + hi, :]
                    b = sb[toff + lo:toff + hi, 0, :]
                if to_dram:
                    dma(eng, d, b, rk, wk + [(key, "d", s)], (key, "s"))
                else:
                    dma(eng, b, d, rk + [(key, "d", s)], wk, (key, "l"))

        def run_layer(l, last_layer):
            gmx = gmix[:, l, :]
            gml = gmlp[:, l, :]
            A.reset(base_mark)
            hb = [A.alloc([128, D], F32) for i in range(2)]
            ab = [A.alloc([128, D], BF16) for i in range(2)]
            ss = A.alloc([128, 24], F32)
            aT = A.alloc([128, KC, TT], BF16)
            q_sb = A.alloc([128, NH, TT], F32)
            e_t = [A.alloc([128, TT], F32) for i in range(2)]
            sig_t = [A.alloc([128, TT], F32) for i in range(2)]
            g_t = [A.alloc([128, TT + 1], F32) for i in range(2)]
            k_t = [A.alloc([128, TT], F32) for i in range(2)]
            b_t = [A.alloc([128, TT], F32) for i in range(2)]
            E1 = [A.alloc([128, TT], F32) for i in range(2)]
            E2 = [A.alloc([128, TT], F32) for i in range(2)]
            qk = {n: A.alloc([128, NH, TT], BF16) for n in ("qf", "kf", "qb", "kb")}
            kT = {n: A.alloc([128, TT // 128, 1024], BF16) for n in ("kf", "kb")}
            v_sb = A.alloc([128, TT // 128, 1024], BF16)
            gate_sb = A.alloc([128, NH, TT], BF16)
            up_sb = A.alloc([128, NH, TT], F32)
            o_sb = A.alloc([128, NH, TT], F32)
            Pf = [A.alloc([128, NH, 64], BF16) for i in range(2)]
            Pb = [A.alloc([128, NH, 64], BF16) for i in range(2)]
            expB = A.alloc([128, TT // 64, NH], F32)
            gtmp = A.alloc([128, 8], F32)
            for s in range(NS):
                memset("pool", Sst[s], 0.0, [("S", s)])
                memset("pool", Sbf[s], 0.0, [("Sbf", s)])
            for i in range(2):
                memset("pool", g_t[i][:, 0:1], 0.0, [("g", i)])

            for tile in tiles:
                T, nb = tile["T"], tile["nb"]
                if tile["head"]:
                    memset("pool", hb[0], 0.0, [("hb", 0)])
                    for s in range(NS):
                        src = meta_d if l == 0 else H1[s][HEAD - NMETA:HEAD, :]
                        dma("sp", hb[0][64 * s + 48:64 * s + 64, :], src, [("H", s)] if l else [], [("hb", 0)], ("hb", 0))
                else:
                    s, slot0 = tile["parts"][0][0], tile["parts"][0][1]
                    for tb in range(nb):
                        if l == 0:
                            src = x_d[s][slot0 - HEAD + tb * 128: slot0 - HEAD + (tb + 1) * 128, :]
                        else:
                            src = H1[s][slot0 + tb * 128: slot0 + (tb + 1) * 128, :]
                        dma("sp", hb[tb % 2], src, [("H", s)] if l else [], [("hb", tb % 2)], ("hb", tb % 2))
                norm_transpose(tile, lambda tb: hb[tb % 2], lambda tb: ("hb", tb % 2), gmx, aT, "aT", ab, ss)

                def silu_evac(dst, dstkey, blk0):
                    def ev(cb, ps, pk):
                        i = cb % 2
                        blk = blk0 + cb
                        P.op("act", lambda e: e.activation(out=e_t[i][:, 0:T], in_=ps, func=AF.Exp, scale=-1.0), r=[pk], w=[("e", i)])
                        P.op("pool", lambda e: e.tensor_scalar(out=e_t[i][:, 0:T], in0=e_t[i][:, 0:T], scalar1=1.0, scalar2=None, op0=ALU.add),
                             r=[("e", i)], w=[("e", i)])
                        P.op("dve", lambda e: e.reciprocal(out=e_t[i][:, 0:T], in_=e_t[i][:, 0:T]), r=[("e", i)], w=[("e", i)])
                        P.op("dve", lambda e: e.tensor_tensor(out=dst[:, blk, 0:T], in0=ps, in1=e_t[i][:, 0:T], op=ALU.mult),
                             r=[pk, ("e", i)], w=[dstkey])
                    return ev
                for j in range(2):
                    slot = load_piece(WIN[l][j], ("WIN", l, j))
                    proj_fm(slot, aT, "aT", T, silu_evac(q_sb, "q", j * 4))

                nch = T // 64
                for dr_ in range(2):
                    for j in range(2):
                        slot = load_piece(WIN[l][2 + 2 * dr_ + j], ("WIN", l, 2 + 2 * dr_ + j))

                        def fev(cb, ps, pk, dr_=dr_, j=j):
                            i = cb % 2
                            h = j * 4 + cb
                            lbe_c = lbe[:, l, dr_, h:h + 1]
                            oml_c = oml[:, l, dr_, h:h + 1]
                            noml_c = noml[:, l, dr_, h:h + 1]
                            et, sg, gt, kt, bt, e1, e2 = e_t[i], sig_t[i], g_t[i], k_t[i], b_t[i], E1[i], E2[i]
                            P.op("act", lambda e: e.activation(out=et[:, 0:T], in_=ps, func=AF.Exp, scale=-1.0), r=[pk], w=[("e", i)])
                            P.op("pool", lambda e: e.tensor_scalar(out=et[:, 0:T], in0=et[:, 0:T], scalar1=1.0, scalar2=None, op0=ALU.add),
                                 r=[("e", i)], w=[("e", i)])
                            P.op("dve", lambda e: e.reciprocal(out=sg[:, 0:T], in_=et[:, 0:T]), r=[("e", i)], w=[("sig", i)])
                            P.op("act", lambda e: e.activation(out=gt[:, 1:T + 1], in_=sg[:, 0:T], func=AF.Ln, scale=oml_c, bias=lbe_c),
                                 r=[("sig", i)], w=[("g", i)])
                            P.op("pool", lambda e: e.tensor_scalar(out=kt[:, 0:T], in0=sg[:, 0:T], scalar1=noml_c, scalar2=oml_c,
                                                                   op0=ALU.mult, op1=ALU.add), r=[("sig", i)], w=[("k", i)])
                            if dr_ == 0:
                                P.op("dve", lambda e: e.tensor_tensor_scan(out=bt[:, 0:T], data0=scanm[:, 0:T], data1=gt[:, 1:T + 1], initial=0.0,
                                                                           op0=ALU.mult, op1=ALU.add), r=[("g", i)], w=[("b", i)])
                            else:
                                P.op("dve", lambda e: e.tensor_tensor_scan(out=bt[:, 0:T], data0=gt[:, 0:T], data1=scanm[:, 0:T], initial=0.0,
                                                                           op0=ALU.add, op1=ALU.mult), r=[("g", i)], w=[("b", i)])
                            P.op("act", lambda e: e.activation(out=e1[:, 0:T], in_=bt[:, 0:T], func=AF.Exp), r=[("b", i)], w=[("E1", i)])
                            P.op("act", lambda e: e.activation(out=e2[:, 0:T], in_=bt[:, 0:T], func=AF.Exp, scale=-1.0), r=[("b", i)], w=[("E2", i)])
                            if dr_ == 0:
                                P.op("pool", lambda e: e.tensor_tensor(out=qk["qf"][:, h, 0:T], in0=q_sb[:, h, 0:T], in1=e1[:, 0:T], op=ALU.mult),
                                     r=["q", ("E1", i)], w=["qf"])
                                P.op("dve", lambda e: e.tensor_tensor(out=qk["kf"][:, h, 0:T], in0=kt[:, 0:T], in1=e2[:, 0:T], op=ALU.mult),
                                     r=[("k", i), ("E2", i)], w=["kf"])
                                P.op("pool", lambda e: e.tensor_copy(out=expB[:, 0:nch, h],
                                                                     in_=e1[:, 0:T].rearrange("p (c t) -> p c t", t=64)[:, :, 63]),
                                     r=[("E1", i)], w=["expB"])
                            else:
                                P.op("pool", lambda e: e.tensor_tensor(out=qk["qb"][:, h, 0:T], in0=q_sb[:, h, 0:T], in1=e2[:, 0:T], op=ALU.mult),
                                     r=["q", ("E2", i)], w=["qb"])
                                P.op("dve", lambda e: e.tensor_tensor(out=qk["kb"][:, h, 0:T], in0=kt[:, 0:T], in1=e1[:, 0:T], op=ALU.mult),
                                     r=[("k", i), ("E1", i)], w=["kb"])
                                P.op("dve", lambda e: e.tensor_tensor(out=gtmp[:, 0:nch],
                                                                      in0=bt[:, 0:T].rearrange("p (c t) -> p c t", t=64)[:, :, 63],
                                                                      in1=gt[:, 1:T + 1].rearrange("p (c t) -> p c t", t=64)[:, :, 63], op=ALU.add),
                                     r=[("b", i), ("g", i)], w=["gtmp"])
                                ci = 0
                                for (s_, c_) in tile["chunks"]:
                                    P.op("act", lambda e, s_=s_, c_=c_, ci=ci: e.activation(out=expG[s_][:, c_, h:h + 1], in_=gtmp[:, ci:ci + 1],
                                                                                          func=AF.Exp), r=["gtmp"], w=[("expG", s_)])
                                    ci += 1
                        proj_fm(slot, aT, "aT", T, fev)

                for j in range(2):
                    slot = load_piece(WIN[l][6 + j], ("WIN", l, 6 + j))

                    def vev(tb, ps, pk, j=j):
                        P.op("act", lambda e: e.activation(out=v_sb[:, tb, j * 512:(j + 1) * 512], in_=ps, func=AF.Copy), r=[pk], w=["v"])
                    proj_tm(slot, aT, "aT", nb, vev)
                for j in range(2):
                    slot = load_piece(WIN[l][8 + j], ("WIN", l, 8 + j))
                    proj_fm(slot, aT, "aT", T, silu_evac(gate_sb, "gate", j * 4))
                for j in range(2):
                    slot = load_piece(WIN[l][10 + j], ("WIN", l, 10 + j))

                    def pev(cb, ps, pk, j=j):
                        P.op("act", lambda e: e.activation(out=up_sb[:, j * 4 + cb, 0:T], in_=ps, func=AF.Copy), r=[pk], w=["up"])
                    proj_fm(slot, aT, "aT", T, pev)

                for n in ("kf", "kb"):
                    for tb in range(nb):
                        for h in range(NH):
                            mm(psT[:, h * 128:(h + 1) * 128], qk[n][:, h, tb * 128:(tb + 1) * 128], ident[:, :], True, True,
                               [n], [("psT", 0), ("psT", 1)])
                        P.op("act" if tb % 2 == 0 else "dve",
                             (lambda e, n=n, tb=tb: e.activation(out=kT[n][:, tb, :], in_=psT[:, :], func=AF.Copy)) if tb % 2 == 0 else
                             (lambda e, n=n, tb=tb: e.tensor_copy(out=kT[n][:, tb, :], in_=psT[:, :])),
                             r=[("psT", 0), ("psT", 1)], w=[n + "T"])

                ci = 0
                for (s_, c_) in tile["chunks"]:
                    tb, half = ci // 2, ci % 2
                    p0 = 64 * half
                    cs = slice(ci * 64, ci * 64 + 64)
                    ps_sc = psT[p0:p0 + 64, :].rearrange("p (d h t) -> p d h t", d=2, h=NH)
                    for h in range(NH):
                        mm(ps_sc[:, 0, h, :], qk["kf"][:, h, cs], qk["qf"][:, h, cs], True, True, ["kf", "qf"], [("psT", 0), ("psT", 1)])
                        mm(ps_sc[:, 1, h, :], qk["kb"][:, h, cs], qk["qb"][:, h, cs], True, True, ["kb", "qb"], [("psT", 0), ("psT", 1)])
                    pi = ci % 2
                    mU = maskU[p0:p0 + 64, :].unsqueeze(1).broadcast_to([64, NH, 64])
                    mL = maskL[p0:p0 + 64, :].unsqueeze(1).broadcast_to([64, NH, 64])
                    P.op("dve", lambda e, ps_sc=ps_sc, pi=pi, p0=p0, mU=mU: e.tensor_tensor(out=Pf[pi][p0:p0 + 64, :, :], in0=ps_sc[:, 0, :, :], in1=mU,
                                                                                            op=ALU.mult), r=[("psT", 0), ("psT", 1)], w=[("Pf", pi)])
                    P.op("dve", lambda e, ps_sc=ps_sc, pi=pi, p0=p0, mL=mL: e.tensor_tensor(out=Pb[pi][p0:p0 + 64, :, :], in0=ps_sc[:, 1, :, :], in1=mL,
                                                                                            op=ALU.mult), r=[("psT", 0), ("psT", 1)], w=[("Pb", pi)])
                    pso = psmm[0][:, :].rearrange("p (h t) -> p h t", h=NH)
                    for h in range(NH):
                        hs = slice(h * 128, (h + 1) * 128)
                        mm(pso[:, h, :], Sbf[s_][:, h, :], qk["qf"][:, h, cs], True, False, [("Sbf", s_), "qf"], [("psmm", 0)])
                        mm(pso[:, h, :], v_sb[p0:p0 + 64, tb, hs], Pf[pi][p0:p0 + 64, h, :], False, False, ["v", ("Pf", pi)], [("psmm", 0)])
                        mm(pso[:, h, :], v_sb[p0:p0 + 64, tb, hs], Pb[pi][p0:p0 + 64, h, :], False, True, ["v", ("Pb", pi)], [("psmm", 0)])
                    P.op("act", lambda e, cs=cs, pso=pso: e.activation(out=o_sb[:, :, cs], in_=pso, func=AF.Copy), r=[("psmm", 0)], w=["o"])
                    psU = psX[:, :].rearrange("p (h v) -> p h v", h=NH)
                    for h in range(NH):
                        hs = slice(h * 128, (h + 1) * 128)
                        mm(psU[:, h, :], kT["kf"][p0:p0 + 64, tb, hs], v_sb[p0:p0 + 64, tb, hs], True, True, ["kfT", "v"], ["psX"])
                    P.op("dve", lambda e, s_=s_, psU=psU: e.tensor_tensor(out=Sst[s_], in0=psU, in1=Sst[s_], op=ALU.add), r=["psX"], w=[("S", s_)])
                    eb = expB[:, ci, :].unsqueeze(2).broadcast_to([128, NH, 128])
                    P.op("pool", lambda e, s_=s_, eb=eb: e.tensor_tensor(out=Sst[s_], in0=Sst[s_], in1=eb, op=ALU.mult), r=["expB"], w=[("S", s_)])
                    P.op("pool", lambda e, s_=s_: e.tensor_copy(out=Sbf[s_], in_=Sst[s_]), r=[("S", s_)], w=[("Sbf", s_)])
                    ci += 1

                if not tile["head"] and tile["parts"][0][0] == 0 and l == 0:
                    dbg("aT", aT, ["aT"], [128, KC, TT], BF16)
                    dbg("q", q_sb, ["q"], [128, NH, TT], F32)
                    dbg("v", v_sb, ["v"], [128, TT // 128, 1024], BF16)
                    for n_ in ("qf", "kf", "qb", "kb"):
                        dbg(n_, qk[n_], [n_], [128, NH, TT], BF16)
                    dbg("kfT", kT["kf"], ["kfT"], [128, TT // 128, 1024], BF16)
                    dbg("o1", o_sb, ["o"], [128, NH, TT], F32)
                    dbg("gate", gate_sb, ["gate"], [128, NH, TT], BF16)
                    dbg("up", up_sb, ["up"], [128, NH, TT], F32)
                    dbg("ident", ident, [], [128, 128], BF16)
                    dbg("maskU", maskU, [], [128, 64], F32)
                    dbg("maskL", maskL, [], [128, 64], F32)
                    dbg("lb", lb.rearrange("p a b h -> p (a b h)"), [], [128, 32], F32)
                fm_dma("sp", OP, o_sb, tile, True, "OP", ["o"], [])
                fm_dma("sp", QB, qk["qb"], tile, True, "QB", ["qb"], [])
                fm_dma("sp", GT, gate_sb, tile, True, "GT", ["gate"], [])
                tm_dma("sp", KBT, kT["kb"], tile, True, "KBT", ["kbT"], [])
                tm_dma("sp", VT, v_sb, tile, True, "VT", ["v"], [])
                for (s, slot0, ntok, toff) in tile["parts"]:
                    dma("sp", UP[s][:, :, 8 + slot0: 8 + slot0 + ntok].rearrange("h p t -> p h t"), up_sb[:, :, toff:toff + ntok],
                        ["up", ("UPpad", s)], [("UP", "d", s)], ("UP", "s"))

            P.barrier()
            A.reset(base_mark)
            hT = A.alloc([128, TT // 128, D], F32)
            ab = [A.alloc([128, D], BF16) for i in range(2)]
            ss = A.alloc([128, 24], F32)
            mixT = A.alloc([128, KC, TT], BF16)
            mT = A.alloc([128, KC, TT], BF16)
            hid = [A.alloc([128, KC, TT], BF16) for i in range(2)]
            rl = [A.alloc([128, TT], F32) for i in range(2)]
            o_sb = A.alloc([128, NH, TT], F32)
            qb_sb = A.alloc([128, NH, TT], BF16)
            kbT_sb = A.alloc([128, TT // 128, 1024], BF16)
            v_sb = A.alloc([128, TT // 128, 1024], BF16)
            gate_sb = A.alloc([128, NH, TT], BF16)
            upx = A.alloc([128, NH, TT + 16], F32)
            pooled = A.alloc([128, NH, TT], BF16)
            pa = {en: [A.alloc([128, TT + 16], F32) for i in range(4)] for en in ("pool", "dve")}
            osq = [A.alloc([128, TT], BF16) for i in range(2)]
            rst = [A.alloc([128, TT], F32) for i in range(2)]
            ytm = [A.alloc([128, TT], F32) for i in range(2)]
            yout = [A.alloc([128, D], F32)] * 2
            for s in range(NS):
                memset("pool", Sst[s], 0.0, [("S", s)])

            for tile in reversed(tiles):
                T, nb = tile["T"], tile["nb"]
                if tile["head"] and last_layer:
                    continue
                nch = T // 64
                fm_dma("sp", OP, o_sb, tile, False, "OP", [], ["o"])
                fm_dma("sp", QB, qb_sb, tile, False, "QB", [], ["qb"])
                fm_dma("sp", GT, gate_sb, tile, False, "GT", [], ["gate"])
                tm_dma("sp", KBT, kbT_sb, tile, False, "KBT", [], ["kbT"])
                tm_dma("sp", VT, v_sb, tile, False, "VT", [], ["v"])
                for pidx, (s, slot0, ntok, toff) in enumerate(tile["parts"]):
                    dma("sp", upx[:, :, toff + 16 * pidx:toff + 16 * pidx + ntok + 16], UP[s][:, :, slot0:slot0 + ntok + 16].rearrange("h p t -> p h t"),
                        [("UP", "d", s), ("UPpad", s)], ["upx"], ("UP", "l"))
                if tile["head"]:
                    memset("pool", hT[:, 0, :], 0.0, [("hT", 0)])
                    for s in range(NS):
                        src = meta_d if l == 0 else H1[s][HEAD - NMETA:HEAD, :]
                        dma("sp", hT[64 * s + 48:64 * s + 64, 0, :], src, [("H", s)] if l else [], [("hT", 0)], ("hT", 0))
                else:
                    s, slot0 = tile["parts"][0][0], tile["parts"][0][1]
                    for tb in range(nb):
                        if l == 0:
                            src = x_d[s][slot0 - HEAD + tb * 128: slot0 - HEAD + (tb + 1) * 128, :]
                        else:
                            src = H1[s][slot0 + tb * 128: slot0 + (tb + 1) * 128, :]
                        dma("sp", hT[:, tb, :], src, [("H", s)] if l else [], [("hT", tb)], ("hT", tb))

                for ci in reversed(range(nch)):
                    s_, c_ = tile["chunks"][ci]
                    tb, half = ci // 2, ci % 2
                    p0 = 64 * half
                    cs = slice(ci * 64, ci * 64 + 64)
                    eg = expG[s_][:, c_, :].unsqueeze(2).broadcast_to([128, NH, 128])
                    P.op("pool", lambda e, s_=s_, eg=eg: e.tensor_tensor(out=Sst[s_], in0=Sst[s_], in1=eg, op=ALU.mult), r=[("expG", s_)], w=[("S", s_)])
                    P.op("pool", lambda e, s_=s_: e.tensor_copy(out=Sbf[s_], in_=Sst[s_]), r=[("S", s_)], w=[("Sbf", s_)])
                    psi = psmm[0][:, :].rearrange("p (h t) -> p h t", h=NH)
                    for h in range(NH):
                        mm(psi[:, h, :], Sbf[s_][:, h, :], qb_sb[:, h, cs], True, True, [("Sbf", s_), "qb"], [("psmm", 0)])
                    P.op("dve", lambda e, cs=cs, psi=psi: e.tensor_tensor(out=o_sb[:, :, cs], in0=psi, in1=o_sb[:, :, cs], op=ALU.add),
                         r=[("psmm", 0)], w=["o"])
                    psU = psX[:, :].rearrange("p (h v) -> p h v", h=NH)
                    for h in range(NH):
                        hs = slice(h * 128, (h + 1) * 128)
                        mm(psU[:, h, :], kbT_sb[p0:p0 + 64, tb, hs], v_sb[p0:p0 + 64, tb, hs], True, True, ["kbT", "v"], ["psX"])
                    P.op("dve", lambda e, s_=s_, psU=psU: e.tensor_tensor(out=Sst[s_], in0=psU, in1=Sst[s_], op=ALU.add), r=["psX"], w=[("S", s_)])

                for h in range(NH):
                    i = h % 2
                    P.op("act", lambda e, h=h, i=i: e.activation(out=osq[i][:, 0:T], in_=o_sb[:, h, 0:T], func=AF.Square), r=["o"], w=[("osq", i)])
                    mm(psmm[1 + i][:, 0:T], ones[:, :], osq[i][:, 0:T], True, True, [("osq", i)], [("psmm", 1 + i)])
                    P.op("act", lambda e, i=i: e.activation(out=rst[i][:, 0:T], in_=psmm[1 + i][:, 0:T], func=AF.Ln, scale=1.0 / 128, bias=epsc[:, 0:1]),
                         r=[("psmm", 1 + i)], w=[("rst", i)])
                    P.op("act", lambda e, i=i: e.activation(out=rst[i][:, 0:T], in_=rst[i][:, 0:T], func=AF.Exp, scale=-0.5), r=[("rst", i)], w=[("rst", i)])
                    P.op("dve", lambda e, h=h, i=i: e.scalar_tensor_tensor(out=ytm[i][:, 0:T], in0=o_sb[:, h, 0:T], scalar=hng[:, l, h:h + 1],
                                                                          in1=rst[i][:, 0:T], op0=ALU.mult, op1=ALU.mult),
                         r=["o", ("rst", i)], w=[("ytm", i)])
                    P.op("pool", lambda e, h=h, i=i: e.tensor_tensor(out=mixT[:, h, 0:T], in0=ytm[i][:, 0:T], in1=gate_sb[:, h, 0:T], op=ALU.mult),
                         r=[("ytm", i), "gate"], w=["mixT"])

                for pidx, (s, slot0, ntok, toff) in enumerate(tile["parts"]):
                    n = ntok + 16
                    for j in range(NH):
                        g = j // 2
                        w = POOLW[g]
                        x = upx[:, j, toff + 16 * pidx:toff + 16 * pidx + n]
                        eng = "pool" if j % 2 == 0 else "dve"
                        a1, a2, a3, a4 = [t[:, 0:n] for t in pa[eng]]
                        pk = [("pa", eng)]
                        P.op(eng, lambda e, x=x, a1=a1, n=n: e.tensor_tensor(out=a1[:, 1:n], in0=x[:, 0:n - 1], in1=x[:, 1:n], op=ALU.add), r=["upx"], w=pk)
                        cur = a1
                        if g >= 1:
                            P.op(eng, lambda e, a1=a1, a2=a2, n=n: e.tensor_tensor(out=a2[:, 2:n - 1], in0=a1[:, 1:n - 2], in1=a1[:, 3:n], op=ALU.add), r=pk, w=pk)
                            cur = a2
                        if g >= 2:
                            P.op(eng, lambda e, a2=a2, a3=a3, n=n: e.tensor_tensor(out=a3[:, 4:n - 3], in0=a2[:, 2:n - 5], in1=a2[:, 6:n - 1], op=ALU.add), r=pk, w=pk)
                            cur = a3
                        if g >= 3:
                            P.op(eng, lambda e, a3=a3, a4=a4, n=n: e.tensor_tensor(out=a4[:, 8:n - 7], in0=a3[:, 4:n - 11], in1=a3[:, 12:n - 3], op=ALU.add), r=pk, w=pk)
                            cur = a4
                        P.op("dve", lambda e, cur=cur, x=x, w=w, j=j, toff=toff, ntok=ntok: e.scalar_tensor_tensor(
                            out=pooled[:, j, toff:toff + ntok], in0=cur[:, 8:8 + ntok], scalar=1.0 / w, in1=x[:, 8:8 + ntok],
                            op0=ALU.mult, op1=ALU.subtract), r=pk + ["upx"], w=["pooled"])
                        fix = []
                        if tile["head"]:
                            for p in range(w // 2):
                                fix.append((48 + p, p + w - w // 2))
                        if tile["last"]:
                            for jj in range(1, w - w // 2):
                                fix.append((ntok - jj, w // 2 + jj))
                        for (col, cntv) in fix:
                            P.op("dve", lambda e, cur=cur, x=x, col=col, cntv=cntv, j=j, toff=toff: e.scalar_tensor_tensor(
                                out=pooled[:, j, toff + col:toff + col + 1], in0=cur[:, 8 + col:9 + col], scalar=1.0 / cntv,
                                in1=x[:, 8 + col:9 + col], op0=ALU.mult, op1=ALU.subtract), r=pk + ["upx"], w=["pooled"])
                for g in range(4):
                    for ob in range(2):
                        bank = 1 + (g * 2 + ob) % 3
                        for kc in range(2):
                            mm(psmm[bank][:, 0:T], wpool[:, l, g, kc, ob * 128:(ob + 1) * 128], pooled[:, 2 * g + kc, 0:T], kc == 0, kc == 1,
                               ["pooled"], [("psmm", bank)])
                        P.op("act", lambda e, g=g, ob=ob, bank=bank: e.mul(out=mixT[:, 8 + 2 * g + ob, 0:T], in_=psmm[bank][:, 0:T],
                                                                           mul=psc[:, l, 2 * g + ob: 2 * g + ob + 1]),
                             r=[("psmm", bank)], w=["mixT"])

                if not tile["head"] and tile["parts"][0][0] == 0 and l == 0:
                    dbg("o2", o_sb, ["o"], [128, NH, TT], F32)
                    dbg("mixT", mixT, ["mixT"], [128, KC, TT], BF16)
                    dbg("pooled", pooled, ["pooled"], [128, NH, TT], BF16)
                for pn in range(4):
                    slot = load_piece(WOUT[l][pn], ("WOUT", l, pn))

                    def oev(tb, ps, pk, pn=pn):
                        P.op("dve", lambda e: e.tensor_tensor(out=hT[:, tb, pn * 512:(pn + 1) * 512], in0=ps, in1=hT[:, tb, pn * 512:(pn + 1) * 512],
                                                              op=ALU.add), r=[pk], w=[("hT", tb)])
                    proj_tm(slot, mixT, "mixT", nb, oev)
                if not tile["head"] and tile["parts"][0][0] == 0 and l == 0:
                    dbg("h1", hT, [("hT", 0), ("hT", 1)], [128, TT // 128, D], F32)
                norm_transpose(tile, lambda tb: hT[:, tb, :], lambda tb: ("hT", tb), gml, mT, "mT", ab, ss)
                for fg in range(4):
                    hd = hid[fg % 2]
                    hk = ("hid", fg % 2)
                    for j in range(4):
                        slot = load_piece(WUP[l][fg * 4 + j], ("WUP", l, fg * 4 + j))

                        def uev(cb, ps, pk, j=j, hd=hd, hk=hk):
                            i = cb % 2
                            P.op("act", lambda e: e.activation(out=rl[i][:, 0:T], in_=ps, func=AF.Relu), r=[pk], w=[("rl", i)])
                            P.op("pool", lambda e: e.tensor_tensor(out=hd[:, j * 4 + cb, 0:T], in0=rl[i][:, 0:T], in1=rl[i][:, 0:T], op=ALU.mult),
                                 r=[("rl", i)], w=[hk])
                        proj_fm(slot, mT, "mT", T, uev)
                    for pn in range(4):
                        slot = load_piece(WDN[l][fg * 4 + pn], ("WDN", l, fg * 4 + pn))

                        def dev(tb, ps, pk, pn=pn):
                            P.op("dve", lambda e: e.tensor_tensor(out=hT[:, tb, pn * 512:(pn + 1) * 512], in0=ps, in1=hT[:, tb, pn * 512:(pn + 1) * 512],
                                                                  op=ALU.add), r=[pk], w=[("hT", tb)])
                        proj_tm(slot, hd, hk, nb, dev)
                if not last_layer:
                    if tile["head"]:
                        for s in range(NS):
                            dma("sp", H1[s][HEAD - NMETA:HEAD, :], hT[64 * s + 48:64 * s + 64, 0, :], [("hT", 0)], [("H", s)], ("H", "s"))
                    else:
                        s, slot0 = tile["parts"][0][0], tile["parts"][0][1]
                        for tb in range(nb):
                            dma("sp", H1[s][slot0 + tb * 128: slot0 + (tb + 1) * 128, :], hT[:, tb, :], [("hT", tb)], [("H", s)], ("H", "s", tb))
                else:
                    s, slot0 = tile["parts"][0][0], tile["parts"][0][1]
                    for tb in range(nb):
                        i = 0
                        hsrc = hT[:, tb, :]
                        P.op("act", lambda e, hsrc=hsrc, i=i, tb=tb: e.activation(out=yout[i], in_=hsrc, func=AF.Square, accum_out=ss[:, tb:tb + 1]),
                             r=[("hT", tb)], w=[("yout", i), ("ss", tb)])
                        P.op("act", lambda e, tb=tb: e.activation(out=ss[:, 8 + tb:9 + tb], in_=ss[:, tb:tb + 1], func=AF.Ln, scale=1.0 / D, bias=epsc[:, 0:1]),
                             r=[("ss", tb)], w=[("ss", tb)])
                        P.op("act", lambda e, tb=tb: e.activation(out=ss[:, 16 + tb:17 + tb], in_=ss[:, 8 + tb:9 + tb], func=AF.Exp, scale=-0.5),
                             r=[("ss", tb)], w=[("ss", tb)])
                        P.op("dve", lambda e, hsrc=hsrc, i=i, tb=tb: e.scalar_tensor_tensor(out=yout[i], in0=hsrc, scalar=ss[:, 16 + tb:17 + tb], in1=fgain,
                                                                                        op0=ALU.mult, op1=ALU.mult), r=[("hT", tb), ("ss", tb)], w=[("yout", i)])
                        dma("sp", y_d[s][slot0 - HEAD + tb * 128: slot0 - HEAD + (tb + 1) * 128, :], yout[i], [("yout", i)], [("Y", s)], ("Y", i))
            P.barrier()

        for l in range(debug_layers):
            run_layer(l, l == debug_layers - 1)
        P.barrier()

        P.finalize(nc, stack)
        DEBUG['_P'] = P
        with nc.Block() as block:
            P.replay(nc, block)
    return nc


_CACHE = {}


def kernel(x_prompt, x_sample, meta_tokens, w_in, w_pool, pool_scale, hg_lower_bound, hg_head_norm, w_out,
           norm_mix, norm_mlp, w_up, w_down, final_norm):
    x_prompt = np.asarray(x_prompt, dtype=np.float32)
    x_sample = np.asarray(x_sample, dtype=np.float32)
    SL, SS = x_prompt.shape[1], x_sample.shape[1]
    nprompt, nsample = x_prompt.shape[0], x_sample.shape[0]
    key = (SL, SS)
    if key not in _CACHE:
        _CACHE[key] = build_program([SL, SS])
    nc = _CACHE[key]
    shared = dict(meta_tokens=meta_tokens, w_in=w_in, w_pool=w_pool, pool_scale=pool_scale, hg_lower_bound=hg_lower_bound,
                  hg_head_norm=hg_head_norm, w_out=w_out, norm_mix=norm_mix, norm_mlp=norm_mlp, w_up=w_up, w_down=w_down,
                  final_norm=final_norm)
    shared = {k: np.ascontiguousarray(np.asarray(v, dtype=np.float32)) for k, v in shared.items()}
    zeros_long = np.zeros((SL, D), np.float32)
    in_maps = []
    for c in range(8):
        m = dict(shared)
        m["x0"] = np.ascontiguousarray(x_prompt[c]) if c < nprompt else zeros_long
        m["x1"] = np.ascontiguousarray(x_sample[c % nsample])
        in_maps.append(m)
    res = run_bass_kernel_spmd(nc, in_maps, core_ids=list(range(8)))
    y_prompt = np.stack([res.results[c]["y0"] for c in range(nprompt)], axis=0)
    y_sample = np.stack([res.results[c]["y1"] for c in range(nsample)], axis=0)
    return (y_prompt.astype(np.float32), y_sample.astype(np.float32))
```

```python
import numpy as np
from contextlib import ExitStack
import concourse.bass as bass
import concourse.mybir as mybir
from concourse.bass_utils import run_bass_kernel_spmd

F32 = mybir.dt.float32
BF16 = mybir.dt.bfloat16
U8 = mybir.dt.uint8
AF = mybir.ActivationFunctionType
ALU = mybir.AluOpType

D = 2048
KC = 16
NH = 8
DFF = 8192
EPS = 1e-6
NMETA = 16
HEAD = 64
TT = 256
POOLW = (2, 4, 8, 16)
ENG = ("pe", "act", "dve", "pool", "sp")
SAME_SYNC = True


class Ins:
    __slots__ = ("eng", "fn", "dkey", "signal", "deps", "sem", "val", "inc", "waits", "nonc")


class Rec:
    def __init__(self):
        self.call = None

    def __getattr__(self, name):
        def f(*a, **k):
            self.call = (name, a, k)
            return self
        return f


class Prog:
    def __init__(self):
        self.q = {e: [] for e in ENG}
        self.all = []
        self.lastw = {}
        self.readers = {}
        self.last_dma = {}

    def op(self, eng, fn, r=(), w=(), dkey=None, extra=()):
        ins = Ins()
        ins.nonc = False
        if fn is not None:
            rec = Rec()
            fn(rec)
            fn = rec.call
        ins.eng, ins.fn, ins.dkey = eng, fn, dkey
        ins.signal = dkey is not None
        deps = {}
        for k in r:
            x = self.lastw.get(k)
            if x is not None:
                deps[id(x)] = x
        for k in w:
            x = self.lastw.get(k)
            if x is not None:
                deps[id(x)] = x
            for y in self.readers.get(k, {}).values():
                deps[id(y)] = y
        for x in extra:
            deps[id(x)] = x
        for k in r:
            self.readers.setdefault(k, {})[eng if dkey is None else ("dma", dkey, len(self.all))] = ins
        for k in w:
            self.lastw[k] = ins
            self.readers[k] = {}
        ins.deps = []
        for d in deps.values():
            if d is ins:
                continue
            if d.dkey is None and d.eng == eng and not (SAME_SYNC and eng in ("act", "dve", "pool")):
                continue
            d.signal = True
            ins.deps.append(d)
        self.q[eng].append(ins)
        self.all.append(ins)
        if dkey is not None:
            self.last_dma[dkey] = ins
        return ins

    def barrier(self):
        lasts = [self.q[e][-1] for e in ENG if self.q[e]] + list(self.last_dma.values())
        for e in ENG:
            self.op(e, None, extra=lasts)

    def finalize(self, nc, stack):
        sems = {e: stack.enter_context(nc.semaphore("s_" + e)) for e in ENG}
        cnt = {e: 0 for e in ENG}
        dcnt = {}
        for ins in self.all:
            if ins.dkey is not None:
                if ins.dkey not in sems:
                    sems[ins.dkey] = stack.enter_context(nc.semaphore("d%d" % len(sems)))
                    dcnt[ins.dkey] = 0
                dcnt[ins.dkey] += 16
                ins.sem, ins.val, ins.inc = sems[ins.dkey], dcnt[ins.dkey], 16
            elif ins.signal and ins.fn is not None:
                cnt[ins.eng] += 1
                ins.sem, ins.val, ins.inc = sems[ins.eng], cnt[ins.eng], 1
            elif ins.signal:
                ins.sem = None
        for e in ENG:
            waited = {}
            for ins in self.q[e]:
                ws = []
                stack_ = list(ins.deps)
                while stack_:
                    d = stack_.pop()
                    if d.fn is None:
                        stack_.extend(d.deps)
                        continue
                    key = id(d.sem)
                    if waited.get(key, 0) < d.val:
                        waited[key] = d.val
                        ws.append((d.sem, d.val))
                best = {}
                for s, v in ws:
                    if id(s) not in best or best[id(s)][1] < v:
                        best[id(s)] = (s, v)
                ins.waits = list(best.values())
        self.nsem = len(sems)

    def replay(self, nc, block):
        engs = {"pe": block.tensor, "act": block.scalar, "dve": block.vector, "pool": block.gpsimd, "sp": block.sync}
        for e in ENG:
            lst = self.q[e]

            def body(eng, lst=lst):
                for ins in lst:
                    for s, v in ins.waits:
                        eng.wait_ge(s, v)
                    if ins.fn is None:
                        continue
                    name, a, k = ins.fn
                    if ins.nonc:
                        with nc.allow_non_contiguous_dma(reason="small strided transfer"):
                            r = getattr(eng, name)(*a, **k)
                    else:
                        r = getattr(eng, name)(*a, **k)
                    if ins.signal:
                        r.then_inc(ins.sem, ins.inc)
            engs[e](body)


class Arena:
    def __init__(self, t, size):
        self.t, self.size, self.off = t, size, 0

    def mark(self):
        return self.off

    def reset(self, m):
        self.off = m

    def alloc(self, shape, dt):
        n = 1
        for s in shape[1:]:
            n *= s
        nb = n * (4 if dt == F32 else 2)
        nb = (nb + 63) // 64 * 64
        assert self.off + nb <= self.size, ("arena overflow", self.off, nb, self.size)
        v = self.t[:, self.off:self.off + nb].bitcast(dt)
        v = v[:, 0:n]
        self.off += nb
        if len(shape) == 3:
            v = v.rearrange("p (a b) -> p a b", a=shape[1])
        elif len(shape) == 4:
            v = v.rearrange("p (a b c) -> p a b c", a=shape[1], b=shape[2])
        return v


DEBUG = {}


def build_program(SEGS, debug_layers=2, debug=False):
    nc = bass.Bass("TRN2", target_bir_lowering=False)
    NS = len(SEGS)
    LT = [HEAD + s for s in SEGS]
    NCH = [l // 64 for l in LT]
    P = Prog()

    def dram(name, shape, dt, kind=None):
        if kind:
            return nc.dram_tensor(name, shape, dt, kind=kind).ap()
        return nc.dram_tensor(name, shape, dt).ap()

    x_d = [dram("x%d" % s, [SEGS[s], D], F32, "ExternalInput") for s in range(NS)]
    y_d = [dram("y%d" % s, [SEGS[s], D], F32, "ExternalOutput") for s in range(NS)]
    meta_d = dram("meta_tokens", [NMETA, D], F32, "ExternalInput")
    w_in_d = dram("w_in", [2, D, 6144], F32, "ExternalInput")
    w_pool_d = dram("w_pool", [2, 4, 256, 256], F32, "ExternalInput")
    pscale_d = dram("pool_scale", [2, 1024], F32, "ExternalInput")
    lbraw_d = dram("hg_lower_bound", [2, 2, 1024], F32, "ExternalInput")
    hn_d = dram("hg_head_norm", [2, 1024], F32, "ExternalInput")
    w_out_d = dram("w_out", [2, D, D], F32, "ExternalInput")
    nmix_d = dram("norm_mix", [2, D], F32, "ExternalInput")
    nmlp_d = dram("norm_mlp", [2, D], F32, "ExternalInput")
    w_up_d = dram("w_up", [2, D, DFF], F32, "ExternalInput")
    w_dn_d = dram("w_down", [2, DFF, D], F32, "ExternalInput")
    fnorm_d = dram("final_norm", [D], F32, "ExternalInput")

    H1 = [dram("H1_%d" % s, [LT[s], D], F32) for s in range(NS)]
    OP = [dram("OP_%d" % s, [NH, 128, LT[s]], F32) for s in range(NS)]
    QB = [dram("QB_%d" % s, [NH, 128, LT[s]], BF16) for s in range(NS)]
    GT = [dram("GT_%d" % s, [NH, 128, LT[s]], BF16) for s in range(NS)]
    KBT = [dram("KBT_%d" % s, [LT[s], 1024], BF16) for s in range(NS)]
    VT = [dram("VT_%d" % s, [LT[s], 1024], BF16) for s in range(NS)]
    UP = [dram("UP_%d" % s, [NH, 128, LT[s] + 16], F32) for s in range(NS)]
    WIN = [dram("WIN_%d" % l, [12, 128, KC, 512], BF16) for l in range(2)]
    WOUT = [dram("WOUT_%d" % l, [4, 128, KC, 512], BF16) for l in range(2)]
    WUP = [dram("WUP_%d" % l, [16, 128, KC, 512], BF16) for l in range(2)]
    WDN = [dram("WDN_%d" % l, [16, 128, KC, 512], BF16) for l in range(2)]

    tiles = [dict(T=128, nb=1, head=True, parts=[(s, 0, 64, 64 * s) for s in range(NS)],
                  chunks=[(s, 0) for s in range(NS)], last=False)]
    assert NS == 2
    for s in range(NS):
        nt = SEGS[s] // TT
        for i in range(nt):
            tiles.append(dict(T=TT, nb=TT // 128, head=False, parts=[(s, HEAD + i * TT, TT, 0)],
                              chunks=[(s, 1 + (TT // 64) * i + j) for j in range(TT // 64)],
                              last=(i == nt - 1), seg=s))

    stack = ExitStack()
    with stack:
        ARENA_BYTES = 188 * 1024
        arena_t = stack.enter_context(nc.sbuf_tensor("arena", [128, ARENA_BYTES], U8))
        A = Arena(arena_t, ARENA_BYTES)
        psmm = [stack.enter_context(nc.psum_tensor("psmm%d" % i, [128, 512], F32)) for i in range(4)]
        psT = stack.enter_context(nc.psum_tensor("psT", [128, 1024], F32))
        psX = stack.enter_context(nc.psum_tensor("psX", [128, 1024], F32))

        ident = A.alloc([128, 128], BF16)
        ones = A.alloc([128, 128], BF16)
        maskU = A.alloc([128, 64], F32)
        maskL = A.alloc([128, 64], F32)
        scanm = A.alloc([128, TT], F32)
        epsc = A.alloc([128, 2], F32)
        lbr = A.alloc([128, 2, 2, NH], F32)
        lbt = A.alloc([128, 8, NH], F32)
        lb = A.alloc([128, 2, 2, NH], F32)
        oml = A.alloc([128, 2, 2, NH], F32)
        noml = A.alloc([128, 2, 2, NH], F32)
        gmix = A.alloc([128, 2, KC], F32)
        gmlp = A.alloc([128, 2, KC], F32)
        hng = A.alloc([128, 2, NH], F32)
        psc = A.alloc([128, 2, NH], F32)
        fgain = A.alloc([128, D], F32)
        wpool = A.alloc([128, 2, 4, 2, 256], BF16) if False else A.alloc([128, 2 * 4 * 2 * 256], BF16)
        wpool = wpool.rearrange("p (l g k c) -> p l g k c", l=2, g=4, k=2)
        zeros = A.alloc([128, 16], F32)
        Sst = [A.alloc([128, NH, 128], F32) for s in range(NS)]
        Sbf = [A.alloc([128, NH, 128], BF16) for s in range(NS)]
        expG = [A.alloc([128, NCH[s], NH], F32) for s in range(NS)]
        wbuf = [A.alloc([128, KC, 512], BF16) for i in range(2)]
        base_mark = A.mark()

        def dbg(name, ap, keys, shape, dt):
            if not debug or name in DEBUG:
                return
            d = nc.dram_tensor("dbg_" + name, list(shape), dt, kind="ExternalOutput").ap()
            DEBUG[name] = d
            dma("sp", d, ap, keys, [], ("dbg", name))

        def memset(eng, ap, val, w):
            return P.op(eng, lambda e, ap=ap, val=val: e.memset(ap, val), w=w)

        def dma(eng, out, in_, r, w, key, nonc=False):
            ins = P.op(eng, lambda e: e.dma_start(out=out, in_=in_), r=r, w=w, dkey=key)
            ins.nonc = nonc
            return ins

        memset("dve", ident, 0.0, ["ident"])
        P.op("pool", lambda e: e.affine_select(out=ident, in_=ident, pattern=[[-1, 128]], compare_op=ALU.not_equal,
                                               fill=1.0, base=0, channel_multiplier=1), r=["ident"], w=["ident"])
        memset("dve", ones, 1.0, ["ones"])
        memset("dve", maskU, 1.0, ["maskU"])
        memset("dve", maskL, 1.0, ["maskL"])
        for half in range(2):
            sl = slice(64 * half, 64 * half + 64)
            P.op("pool", lambda e, sl=sl, half=half: e.affine_select(
                out=maskU[sl, :], in_=maskU[sl, :], pattern=[[1, 64]], compare_op=ALU.is_ge, fill=0.0,
                base=0, channel_multiplier=-1), r=["maskU"], w=["maskU"])
            P.op("pool", lambda e, sl=sl, half=half: e.affine_select(
                out=maskL[sl, :], in_=maskL[sl, :], pattern=[[-1, 64]], compare_op=ALU.is_ge, fill=0.0,
                base=0, channel_multiplier=1), r=["maskL"], w=["maskL"])
        memset("dve", scanm, 1.0, ["scanm"])
        for c in range(TT // 64):
            memset("dve", scanm[:, c * 64:c * 64 + 1], 0.0, ["scanm"])
        memset("dve", epsc[:, 0:1], EPS, ["epsc"])
        memset("dve", epsc[:, 1:2], 1e-30, ["epsc"])
        memset("dve", zeros, 0.0, ["zeros"])
        dma("sp", lbr.rearrange("p a b h -> p (a b) h"), lbraw_d.rearrange("a b (h p) -> p (a b) h", p=128),
            [], ["lbr"], "par_lbr", nonc=True)
        dma("sp", gmix, nmix_d.rearrange("l (k p) -> p l k", p=128), [], ["gmix"], "par", nonc=True)
        dma("sp", gmlp, nmlp_d.rearrange("l (k p) -> p l k", p=128), [], ["gmlp"], "par", nonc=True)
        dma("sp", hng, hn_d.rearrange("l (h p) -> p l h", p=128), [], ["hng"], "par", nonc=True)
        dma("sp", psc, pscale_d.rearrange("l (h p) -> p l h", p=128), [], ["psc"], "par", nonc=True)
        dma("sp", fgain, fnorm_d.partition_broadcast(128), [], ["fgain"], "par")
        for l in range(2):
            for g in range(4):
                dma("pool", wpool[:, l, g, :, :], w_pool_d[l, g].rearrange("(k p) c -> p k c", p=128),
                    [], ["wpool"], "par2")
        for s in range(NS):
            dma("sp", UP[s][:, :, 0:8].rearrange("h p t -> p h t"), zeros[:, 0:8].unsqueeze(1).broadcast_to([128, NH, 8]),
                ["zeros"], [("UPpad", s)], "par", nonc=True)
            dma("sp", UP[s][:, :, LT[s] + 8:LT[s] + 16].rearrange("h p t -> p h t"),
                zeros[:, 0:8].unsqueeze(1).broadcast_to([128, NH, 8]), ["zeros"], [("UPpad", s)], "par", nonc=True)
        for a in range(2):
            x0, x1 = lbr[:, a, 0, :], lbr[:, a, 1, :]
            m, e0, e1, sm = lbt[:, 0, :], lbt[:, 1, :], lbt[:, 2, :], lbt[:, 3, :]
            P.op("dve", lambda e, x0=x0, x1=x1, m=m: e.tensor_tensor(out=m, in0=x0, in1=x1, op=ALU.max), r=["lbr"], w=["lbt"])
            P.op("dve", lambda e, x0=x0, m=m, e0=e0: e.tensor_tensor(out=e0, in0=x0, in1=m, op=ALU.subtract), r=["lbt"], w=["lbt"])
            P.op("dve", lambda e, x1=x1, m=m, e1=e1: e.tensor_tensor(out=e1, in0=x1, in1=m, op=ALU.subtract), r=["lbt"], w=["lbt"])
            P.op("act", lambda e, e0=e0: e.activation(out=e0, in_=e0, func=AF.Exp), r=["lbt"], w=["lbt"])
            P.op("act", lambda e, e1=e1: e.activation(out=e1, in_=e1, func=AF.Exp), r=["lbt"], w=["lbt"])
            P.op("dve", lambda e, e0=e0, e1=e1, sm=sm: e.tensor_tensor(out=sm, in0=e0, in1=e1, op=ALU.add), r=["lbt"], w=["lbt"])
            P.op("dve", lambda e, sm=sm: e.reciprocal(out=sm, in_=sm), r=["lbt"], w=["lbt"])
            P.op("dve", lambda e, e0=e0, sm=sm: e.tensor_tensor(out=e0, in0=e0, in1=sm, op=ALU.mult), r=["lbt"], w=["lbt"])
            P.op("dve", lambda e, e1=e1, sm=sm: e.tensor_tensor(out=e1, in0=e1, in1=sm, op=ALU.mult), r=["lbt"], w=["lbt"])
            P.op("dve", lambda e, e0=e0, a=a: e.tensor_tensor(out=lb[:, 0, a, :], in0=e0, in1=e0, op=ALU.subtract), r=["lbt"], w=["lb"])
            P.op("dve", lambda e, e0=e0, e1=e1, sm=sm: e.tensor_tensor(out=sm, in0=e0, in1=e1, op=ALU.add), r=["lbt"], w=["lbt"])
            P.op("dve", lambda e, e0=e0, sm=sm, a=a: e.tensor_tensor(out=lb[:, 1, a, :], in0=sm, in1=e0, op=ALU.subtract), r=["lbt"], w=["lb"])
        lbf, omlf, nomlf = [t.rearrange("p a b h -> p (a b h)") for t in (lb, oml, noml)]
        P.op("dve", lambda e: e.tensor_scalar(out=omlf, in0=lbf, scalar1=-1.0, scalar2=1.0, op0=ALU.mult, op1=ALU.add), r=["lb"], w=["oml"])
        P.op("dve", lambda e: e.tensor_scalar(out=nomlf, in0=lbf, scalar1=1.0, scalar2=-1.0, op0=ALU.mult, op1=ALU.add), r=["lb"], w=["oml"])
        lbe = A.alloc([128, 2, 2, NH], F32)
        base_mark = A.mark()
        P.op("dve", lambda e: e.scalar_tensor_tensor(out=lbe.rearrange("p a b h -> p (a b h)"), in0=omlf, scalar=1e-30, in1=lbf,
                                                     op0=ALU.mult, op1=ALU.add), r=["lb", "oml"], w=["lbe"])

        def conv(l):
            wi = w_in_d[l].rearrange("(k p) c -> p k c", p=128)
            for j in range(12):
                dma("pool", WIN[l][j], wi[:, :, j * 512:(j + 1) * 512], [], [("WIN", l, j)], ("cv", j % 4))
            wo = w_out_d[l].rearrange("(k p) c -> p k c", p=128)
            for j in range(4):
                dma("pool", WOUT[l][j], wo[:, :, j * 512:(j + 1) * 512], [], [("WOUT", l, j)], ("cv", j % 4))
            wu = w_up_d[l].rearrange("(k p) c -> p k c", p=128)
            for j in range(16):
                dma("pool", WUP[l][j], wu[:, :, j * 512:(j + 1) * 512], [], [("WUP", l, j)], ("cv", j % 4))
            wd = w_dn_d[l].rearrange("(g k p) c -> g p k c", p=128, k=KC)
            for fg in range(4):
                for pn in range(4):
                    dma("pool", WDN[l][fg * 4 + pn], wd[fg][:, :, pn * 512:(pn + 1) * 512], [], [("WDN", l, fg * 4 + pn)], ("cv", pn))
        conv(0)
        if debug_layers > 1:
            conv(1)
        P.barrier()

        wstate = {"i": 0}

        def load_piece(dram_piece, key):
            slot = wstate["i"] % 2
            wstate["i"] += 1
            dma("sp", wbuf[slot], dram_piece, [key], [("wbuf", slot)], ("wb", slot))
            return slot

        def mm(out, lhsT, rhs, start, stop, r, w):
            return P.op("pe", lambda e: e.matmul(out, lhsT=lhsT, rhs=rhs, start=start, stop=stop), r=r, w=w)

        def norm_transpose(tile, src, srckey, gain, aT, aTkey, ab, ss):
            for tb in range(tile["nb"]):
                slot = tb % 2
                hsrc = src(tb)
                abk = ("ab", slot)
                P.op("act", lambda e, hsrc=hsrc, slot=slot, tb=tb: e.activation(
                    out=ab[slot], in_=hsrc, func=AF.Square, accum_out=ss[:, tb:tb + 1]), r=[srckey(tb)], w=[abk, ("ss", tb)])
                P.op("act", lambda e, tb=tb: e.activation(out=ss[:, 8 + tb:9 + tb], in_=ss[:, tb:tb + 1], func=AF.Ln,
                                                          scale=1.0 / D, bias=epsc[:, 0:1]), r=[("ss", tb)], w=[("ss", tb)])
                P.op("act", lambda e, tb=tb: e.activation(out=ss[:, 16 + tb:17 + tb], in_=ss[:, 8 + tb:9 + tb], func=AF.Exp,
                                                          scale=-0.5), r=[("ss", tb)], w=[("ss", tb)])
                P.op("act", lambda e, hsrc=hsrc, slot=slot, tb=tb: e.mul(
                    out=ab[slot], in_=hsrc, mul=ss[:, 16 + tb:17 + tb]), r=[srckey(tb), ("ss", tb)], w=[abk])
                for k4 in range(4):
                    half = k4 % 2
                    pk = ("psT", half)
                    for j in range(4):
                        kc = k4 * 4 + j
                        mm(psT[:, half * 512 + j * 128: half * 512 + (j + 1) * 128], ab[slot][:, kc * 128:(kc + 1) * 128], ident[:, :],
                           True, True, [abk], [pk])
                    gb = gain[:, k4 * 4:(k4 + 1) * 4].unsqueeze(2).broadcast_to([128, 4, 128])
                    P.op("dve", lambda e, half=half, k4=k4, tb=tb, gb=gb: e.tensor_tensor(
                        out=aT[:, k4 * 4:(k4 + 1) * 4, tb * 128:(tb + 1) * 128],
                        in0=psT[:, half * 512:(half + 1) * 512].rearrange("p (a b) -> p a b", a=4), in1=gb, op=ALU.mult),
                        r=[pk], w=[aTkey])

        def proj_fm(slot, aT, aTkey, T, evac, kcs=KC, nbanks=4):
            for cb in range(4):
                bank = cb % nbanks
                pk = ("psmm", bank)
                for kc in range(kcs):
                    mm(psmm[bank][:, 0:T], wbuf[slot][:, kc, cb * 128:(cb + 1) * 128], aT[:, kc, 0:T], kc == 0, kc == kcs - 1,
                       [("wbuf", slot), aTkey], [pk])
                evac(cb, psmm[bank][:, 0:T], pk)

        def proj_tm(slot, actT, actkey, nb, evac, kcs=KC, nbanks=4):
            for tb in range(nb):
                bank = tb % nbanks
                pk = ("psmm", bank)
                for kc in range(kcs):
                    mm(psmm[bank][:, :], actT[:, kc, tb * 128:(tb + 1) * 128], wbuf[slot][:, kc, :], kc == 0, kc == kcs - 1,
                       [("wbuf", slot), actkey], [pk])
                evac(tb, psmm[bank][:, :], pk)

        def fm_dma(eng, dr, sb, tile, to_dram, key, rk, wk, halo=0):
            for (s, slot0, ntok, toff) in tile["parts"]:
                if halo:
                    d = dr[s][:, :, slot0:slot0 + ntok + 2 * halo].rearrange("h p t -> p h t")
                    b = sb[:, :, toff:toff + ntok + 2 * halo]
                else:
                    d = dr[s][:, :, slot0:slot0 + ntok].rearrange("h p t -> p h t")
                    b = sb[:, :, toff:toff + ntok]
                if to_dram:
                    dma(eng, d, b, rk, wk + [(key, "d", s)], (key, "s"))
                else:
                    dma(eng, b, d, rk + [(key, "d", s)], wk, (key, "l"))

        def tm_dma(eng, dr, sb, tile, to_dram, key, rk, wk, rows=None):
            for (s, slot0, ntok, toff) in tile["parts"]:
                if ntok % 128 == 0:
                    d = dr[s][slot0:slot0 + ntok, :].rearrange("(b p) c -> p b c", p=128)
                    b = sb[:, toff // 128: toff // 128 + ntok // 128, :]
                else:
                    lo, hi = (0, ntok) if rows is None else rows
                    d = dr[s][slot0 + lo:slot0 + hi, :]
                    b = sb[toff + lo:toff + hi, 0, :]
                if to_dram:
                    dma(eng, d, b, rk, wk + [(key, "d", s)], (key, "s"))
                else:
                    dma(eng, b, d, rk + [(key, "d", s)], wk, (key, "l"))

        def run_layer(l, last_layer):
            gmx = gmix[:, l, :]
            gml = gmlp[:, l, :]
            A.reset(base_mark)
            hb = [A.alloc([128, D], F32) for i in range(2)]
            ab = [A.alloc([128, D], BF16) for i in range(2)]
            ss = A.alloc([128, 24], F32)
            aT = A.alloc([128, KC, TT], BF16)
            q_sb = A.alloc([128, NH, TT], F32)
            e_t = [A.alloc([128, TT], F32) for i in range(2)]
            sig_t = [A.alloc([128, TT], F32) for i in range(2)]
            g_t = [A.alloc([128, TT + 1], F32) for i in range(2)]
            k_t = [A.alloc([128, TT], F32) for i in range(2)]
            b_t = [A.alloc([128, TT], F32) for i in range(2)]
            E1 = [A.alloc([128, TT], F32) for i in range(2)]
            E2 = [A.alloc([128, TT], F32) for i in range(2)]
            qk = {n: A.alloc([128, NH, TT], BF16) for n in ("qf", "kf", "qb", "kb")}
            kT = {n: A.alloc([128, TT // 128, 1024], BF16) for n in ("kf", "kb")}
            v_sb = A.alloc([128, TT // 128, 1024], BF16)
            gate_sb = A.alloc([128, NH, TT], BF16)
            up_sb = A.alloc([128, NH, TT], F32)
            o_sb = A.alloc([128, NH, TT], F32)
            Pf = [A.alloc([128, NH, 64], BF16) for i in range(2)]
            Pb = [A.alloc([128, NH, 64], BF16) for i in range(2)]
            expB = A.alloc([128, TT // 64, NH], F32)
            gtmp = A.alloc([128, 8], F32)
            DEBUG["mem_p1"] = A.off
            for s in range(NS):
                memset("pool", Sst[s], 0.0, [("S", s)])
                memset("pool", Sbf[s], 0.0, [("Sbf", s)])
            for i in range(2):
                memset("pool", g_t[i][:, 0:1], 0.0, [("g", i)])

            for tile in tiles:
                T, nb = tile["T"], tile["nb"]
                if tile["head"]:
                    memset("pool", hb[0], 0.0, [("hb", 0)])
                    for s in range(NS):
                        src = meta_d if l == 0 else H1[s][HEAD - NMETA:HEAD, :]
                        dma("sp", hb[0][64 * s + 48:64 * s + 64, :], src, [("H", s)] if l else [], [("hb", 0)], ("hb", 0))
                else:
                    s, slot0 = tile["parts"][0][0], tile["parts"][0][1]
                    for tb in range(nb):
                        if l == 0:
                            src = x_d[s][slot0 - HEAD + tb * 128: slot0 - HEAD + (tb + 1) * 128, :]
                        else:
                            src = H1[s][slot0 + tb * 128: slot0 + (tb + 1) * 128, :]
                        dma("sp", hb[tb % 2], src, [("H", s)] if l else [], [("hb", tb % 2)], ("hb", tb % 2))
                norm_transpose(tile, lambda tb: hb[tb % 2], lambda tb: ("hb", tb % 2), gmx, aT, "aT", ab, ss)

                def silu_evac(dst, dstkey, blk0):
                    def ev(cb, ps, pk):
                        i = cb % 2
                        blk = blk0 + cb
                        P.op("act", lambda e: e.activation(out=e_t[i][:, 0:T], in_=ps, func=AF.Exp, scale=-1.0), r=[pk], w=[("e", i)])
                        P.op("dve", lambda e: e.tensor_scalar(out=e_t[i][:, 0:T], in0=e_t[i][:, 0:T], scalar1=1.0, scalar2=None, op0=ALU.add),
                             r=[("e", i)], w=[("e", i)])
                        P.op("dve", lambda e: e.reciprocal(out=e_t[i][:, 0:T], in_=e_t[i][:, 0:T]), r=[("e", i)], w=[("e", i)])
                        P.op("dve", lambda e: e.tensor_tensor(out=dst[:, blk, 0:T], in0=ps, in1=e_t[i][:, 0:T], op=ALU.mult),
                             r=[pk, ("e", i)], w=[dstkey])
                    return ev
                for j in range(2):
                    slot = load_piece(WIN[l][j], ("WIN", l, j))
                    proj_fm(slot, aT, "aT", T, silu_evac(q_sb, "q", j * 4))

                nch = T // 64
                for dr_ in range(2):
                    for j in range(2):
                        slot = load_piece(WIN[l][2 + 2 * dr_ + j], ("WIN", l, 2 + 2 * dr_ + j))

                        def fev(cb, ps, pk, dr_=dr_, j=j):
                            i = cb % 2
                            h = j * 4 + cb
                            lbe_c = lbe[:, l, dr_, h:h + 1]
                            oml_c = oml[:, l, dr_, h:h + 1]
                            noml_c = noml[:, l, dr_, h:h + 1]
                            et, sg, gt, kt, bt, e1, e2 = e_t[i], sig_t[i], g_t[i], k_t[i], b_t[i], E1[i], E2[i]
                            P.op("act", lambda e: e.activation(out=et[:, 0:T], in_=ps, func=AF.Exp, scale=-1.0), r=[pk], w=[("e", i)])
                            P.op("dve", lambda e: e.tensor_scalar(out=et[:, 0:T], in0=et[:, 0:T], scalar1=1.0, scalar2=None, op0=ALU.add),
                                 r=[("e", i)], w=[("e", i)])
                            P.op("dve", lambda e: e.reciprocal(out=sg[:, 0:T], in_=et[:, 0:T]), r=[("e", i)], w=[("sig", i)])
                            P.op("act", lambda e: e.activation(out=gt[:, 1:T + 1], in_=sg[:, 0:T], func=AF.Ln, scale=oml_c, bias=lbe_c),
                                 r=[("sig", i)], w=[("g", i)])
                            P.op("dve", lambda e: e.tensor_scalar(out=kt[:, 0:T], in0=sg[:, 0:T], scalar1=noml_c, scalar2=oml_c,
                                                                   op0=ALU.mult, op1=ALU.add), r=[("sig", i)], w=[("k", i)])
                            if dr_ == 0:
                                P.op("dve", lambda e: e.tensor_tensor_scan(out=bt[:, 0:T], data0=scanm[:, 0:T], data1=gt[:, 1:T + 1], initial=0.0,
                                                                           op0=ALU.mult, op1=ALU.add), r=[("g", i)], w=[("b", i)])
                            else:
                                P.op("dve", lambda e: e.tensor_tensor_scan(out=bt[:, 0:T], data0=gt[:, 0:T], data1=scanm[:, 0:T], initial=0.0,
                                                                           op0=ALU.add, op1=ALU.mult), r=[("g", i)], w=[("b", i)])
                            P.op("act", lambda e: e.activation(out=e1[:, 0:T], in_=bt[:, 0:T], func=AF.Exp), r=[("b", i)], w=[("E1", i)])
                            P.op("act", lambda e: e.activation(out=e2[:, 0:T], in_=bt[:, 0:T], func=AF.Exp, scale=-1.0), r=[("b", i)], w=[("E2", i)])
                            if dr_ == 0:
                                P.op("pool", lambda e: e.tensor_tensor(out=qk["qf"][:, h, 0:T], in0=q_sb[:, h, 0:T], in1=e1[:, 0:T], op=ALU.mult),
                                     r=["q", ("E1", i)], w=["qf"])
                                P.op("dve", lambda e: e.tensor_tensor(out=qk["kf"][:, h, 0:T], in0=kt[:, 0:T], in1=e2[:, 0:T], op=ALU.mult),
                                     r=[("k", i), ("E2", i)], w=["kf"])
                                P.op("pool", lambda e: e.tensor_copy(out=expB[:, 0:nch, h],
                                                                     in_=e1[:, 0:T].rearrange("p (c t) -> p c t", t=64)[:, :, 63]),
                                     r=[("E1", i)], w=["expB"])
                            else:
                                P.op("pool", lambda e: e.tensor_tensor(out=qk["qb"][:, h, 0:T], in0=q_sb[:, h, 0:T], in1=e2[:, 0:T], op=ALU.mult),
                                     r=["q", ("E2", i)], w=["qb"])
                                P.op("dve", lambda e: e.tensor_tensor(out=qk["kb"][:, h, 0:T], in0=kt[:, 0:T], in1=e1[:, 0:T], op=ALU.mult),
                                     r=[("k", i), ("E1", i)], w=["kb"])
                                P.op("dve", lambda e: e.tensor_tensor(out=gtmp[:, 0:nch],
                                                                      in0=bt[:, 0:T].rearrange("p (c t) -> p c t", t=64)[:, :, 63],
                                                                      in1=gt[:, 1:T + 1].rearrange("p (c t) -> p c t", t=64)[:, :, 63], op=ALU.add),
                                     r=[("b", i), ("g", i)], w=["gtmp"])
                                ci = 0
                                for (s_, c_) in tile["chunks"]:
                                    P.op("act", lambda e, s_=s_, c_=c_, ci=ci: e.activation(out=expG[s_][:, c_, h:h + 1], in_=gtmp[:, ci:ci + 1],
                                                                                          func=AF.Exp), r=["gtmp"], w=[("expG", s_)])
                                    ci += 1
                        proj_fm(slot, aT, "aT", T, fev)

                for j in range(2):
                    slot = load_piece(WIN[l][6 + j], ("WIN", l, 6 + j))

                    def vev(tb, ps, pk, j=j):
                        P.op("act", lambda e: e.activation(out=v_sb[:, tb, j * 512:(j + 1) * 512], in_=ps, func=AF.Copy), r=[pk], w=["v"])
                    proj_tm(slot, aT, "aT", nb, vev)
                for j in range(2):
                    slot = load_piece(WIN[l][8 + j], ("WIN", l, 8 + j))
                    proj_fm(slot, aT, "aT", T, silu_evac(gate_sb, "gate", j * 4))
                for j in range(2):
                    slot = load_piece(WIN[l][10 + j], ("WIN", l, 10 + j))

                    def pev(cb, ps, pk, j=j):
                        P.op("act", lambda e: e.activation(out=up_sb[:, j * 4 + cb, 0:T], in_=ps, func=AF.Copy), r=[pk], w=["up"])
                    proj_fm(slot, aT, "aT", T, pev)

                for n in ("kf", "kb"):
                    for tb in range(nb):
                        for h in range(NH):
                            mm(psT[:, h * 128:(h + 1) * 128], qk[n][:, h, tb * 128:(tb + 1) * 128], ident[:, :], True, True,
                               [n], [("psT", 0), ("psT", 1)])
                        P.op("act" if tb % 2 == 0 else "dve",
                             (lambda e, n=n, tb=tb: e.activation(out=kT[n][:, tb, :], in_=psT[:, :], func=AF.Copy)) if tb % 2 == 0 else
                             (lambda e, n=n, tb=tb: e.tensor_copy(out=kT[n][:, tb, :], in_=psT[:, :])),
                             r=[("psT", 0), ("psT", 1)], w=[n + "T"])

                ci = 0
                for (s_, c_) in tile["chunks"]:
                    tb, half = ci // 2, ci % 2
                    p0 = 64 * half
                    cs = slice(ci * 64, ci * 64 + 64)
                    ps_sc = psT[p0:p0 + 64, :].rearrange("p (d h t) -> p d h t", d=2, h=NH)
                    for h in range(NH):
                        mm(ps_sc[:, 0, h, :], qk["kf"][:, h, cs], qk["qf"][:, h, cs], True, True, ["kf", "qf"], [("psT", 0), ("psT", 1)])
                        mm(ps_sc[:, 1, h, :], qk["kb"][:, h, cs], qk["qb"][:, h, cs], True, True, ["kb", "qb"], [("psT", 0), ("psT", 1)])
                    pi = ci % 2
                    mU = maskU[p0:p0 + 64, :].unsqueeze(1).broadcast_to([64, NH, 64])
                    mL = maskL[p0:p0 + 64, :].unsqueeze(1).broadcast_to([64, NH, 64])
                    P.op("dve", lambda e, ps_sc=ps_sc, pi=pi, p0=p0, mU=mU: e.tensor_tensor(out=Pf[pi][p0:p0 + 64, :, :], in0=ps_sc[:, 0, :, :], in1=mU,
                                                                                            op=ALU.mult), r=[("psT", 0), ("psT", 1)], w=[("Pf", pi)])
                    P.op("dve", lambda e, ps_sc=ps_sc, pi=pi, p0=p0, mL=mL: e.tensor_tensor(out=Pb[pi][p0:p0 + 64, :, :], in0=ps_sc[:, 1, :, :], in1=mL,
                                                                                            op=ALU.mult), r=[("psT", 0), ("psT", 1)], w=[("Pb", pi)])
                    pso = psmm[0][:, :].rearrange("p (h t) -> p h t", h=NH)
                    for h in range(NH):
                        hs = slice(h * 128, (h + 1) * 128)
                        mm(pso[:, h, :], Sbf[s_][:, h, :], qk["qf"][:, h, cs], True, False, [("Sbf", s_), "qf"], [("psmm", 0)])
                        mm(pso[:, h, :], v_sb[p0:p0 + 64, tb, hs], Pf[pi][p0:p0 + 64, h, :], False, False, ["v", ("Pf", pi)], [("psmm", 0)])
                        mm(pso[:, h, :], v_sb[p0:p0 + 64, tb, hs], Pb[pi][p0:p0 + 64, h, :], False, True, ["v", ("Pb", pi)], [("psmm", 0)])
                    P.op("act", lambda e, cs=cs, pso=pso: e.activation(out=o_sb[:, :, cs], in_=pso, func=AF.Copy), r=[("psmm", 0)], w=["o"])
                    psU = psX[:, :].rearrange("p (h v) -> p h v", h=NH)
                    for h in range(NH):
                        hs = slice(h * 128, (h + 1) * 128)
                        mm(psU[:, h, :], kT["kf"][p0:p0 + 64, tb, hs], v_sb[p0:p0 + 64, tb, hs], True, True, ["kfT", "v"], ["psX"])
                    P.op("dve", lambda e, s_=s_, psU=psU: e.tensor_tensor(out=Sst[s_], in0=psU, in1=Sst[s_], op=ALU.add), r=["psX"], w=[("S", s_)])
                    eb = expB[:, ci, :].unsqueeze(2).broadcast_to([128, NH, 128])
                    P.op("dve", lambda e, s_=s_, eb=eb: e.tensor_tensor(out=Sst[s_], in0=Sst[s_], in1=eb, op=ALU.mult), r=["expB"], w=[("S", s_)])
                    P.op("act", lambda e, s_=s_: e.activation(out=Sbf[s_], in_=Sst[s_], func=AF.Copy), r=[("S", s_)], w=[("Sbf", s_)])
                    ci += 1

                if not tile["head"] and tile["parts"][0][0] == 0 and l == 0:
                    dbg("aT", aT, ["aT"], [128, KC, TT], BF16)
                    dbg("q", q_sb, ["q"], [128, NH, TT], F32)
                    dbg("v", v_sb, ["v"], [128, TT // 128, 1024], BF16)
                    for n_ in ("qf", "kf", "qb", "kb"):
                        dbg(n_, qk[n_], [n_], [128, NH, TT], BF16)
                    dbg("kfT", kT["kf"], ["kfT"], [128, TT // 128, 1024], BF16)
                    dbg("o1", o_sb, ["o"], [128, NH, TT], F32)
                    dbg("gate", gate_sb, ["gate"], [128, NH, TT], BF16)
                    dbg("up", up_sb, ["up"], [128, NH, TT], F32)
                    dbg("ident", ident, [], [128, 128], BF16)
                    dbg("maskU", maskU, [], [128, 64], F32)
                    dbg("maskL", maskL, [], [128, 64], F32)
                    dbg("lb", lb.rearrange("p a b h -> p (a b h)"), [], [128, 32], F32)
                fm_dma("sp", OP, o_sb, tile, True, "OP", ["o"], [])
                fm_dma("sp", QB, qk["qb"], tile, True, "QB", ["qb"], [])
                fm_dma("sp", GT, gate_sb, tile, True, "GT", ["gate"], [])
                tm_dma("sp", KBT, kT["kb"], tile, True, "KBT", ["kbT"], [])
                tm_dma("sp", VT, v_sb, tile, True, "VT", ["v"], [])
                for (s, slot0, ntok, toff) in tile["parts"]:
                    dma("sp", UP[s][:, :, 8 + slot0: 8 + slot0 + ntok].rearrange("h p t -> p h t"), up_sb[:, :, toff:toff + ntok],
                        ["up", ("UPpad", s)], [("UP", "d", s)], ("UP", "s"))

            P.barrier()
            A.reset(base_mark)
            hT = A.alloc([128, TT // 128, D], F32)
            abraw = A.alloc([128, 2 * D], BF16)
            ab = [abraw[:, 0:D], abraw[:, D:2 * D]]
            yout = abraw.bitcast(F32)
            ss = A.alloc([128, 24], F32)
            mixbuf = [A.alloc([128, KC, TT], BF16) for i in range(2)]
            mT = A.alloc([128, KC, TT], BF16)
            hid = [A.alloc([128, KC, TT], BF16) for i in range(2)]
            rl = [A.alloc([128, TT], F32) for i in range(2)]
            o_sb = A.alloc([128, NH, TT], F32)
            qb_sb = A.alloc([128, NH, TT], BF16)
            kbT_sb = A.alloc([128, TT // 128, 1024], BF16)
            v_sb = A.alloc([128, TT // 128, 1024], BF16)
            gate_sb = A.alloc([128, NH, TT], BF16)
            upx = A.alloc([128, NH, TT + 16], F32)
            pooled = A.alloc([128, NH, TT], BF16)
            pa = {en: [A.alloc([128, TT + 16], F32) for i in range(4)] for en in ("pool", "dve")}
            osq = [A.alloc([128, TT], BF16) for i in range(2)]
            rst = [A.alloc([128, TT], F32) for i in range(2)]
            ytm = [A.alloc([128, TT], F32) for i in range(2)]
            for s in range(NS):
                memset("pool", Sst[s], 0.0, [("S", s)])
            DEBUG["mem_p2"] = A.off

            def gen_M(tile, mixT, mk):
                T, nb = tile["T"], tile["nb"]
                nch = T // 64
                fm_dma("act", OP, o_sb, tile, False, "OP", [], ["o"])
                fm_dma("act", QB, qb_sb, tile, False, "QB", [], ["qb"])
                fm_dma("act", GT, gate_sb, tile, False, "GT", [], ["gate"])
                tm_dma("act", KBT, kbT_sb, tile, False, "KBT", [], ["kbT"])
                tm_dma("act", VT, v_sb, tile, False, "VT", [], ["v"])
                for pidx, (s, slot0, ntok, toff) in enumerate(tile["parts"]):
                    dma("act", upx[:, :, toff + 16 * pidx:toff + 16 * pidx + ntok + 16], UP[s][:, :, slot0:slot0 + ntok + 16].rearrange("h p t -> p h t"),
                        [("UP", "d", s), ("UPpad", s)], ["upx"], ("UP", "l"))
                yield
                for ci in reversed(range(nch)):
                    s_, c_ = tile["chunks"][ci]
                    tb, half = ci // 2, ci % 2
                    p0 = 64 * half
                    cs = slice(ci * 64, ci * 64 + 64)
                    eg = expG[s_][:, c_, :].unsqueeze(2).broadcast_to([128, NH, 128])
                    P.op("dve", lambda e, s_=s_, eg=eg: e.tensor_tensor(out=Sst[s_], in0=Sst[s_], in1=eg, op=ALU.mult), r=[("expG", s_)], w=[("S", s_)])
                    P.op("act", lambda e, s_=s_: e.activation(out=Sbf[s_], in_=Sst[s_], func=AF.Copy), r=[("S", s_)], w=[("Sbf", s_)])
                    psi = psmm[3][:, :].rearrange("p (h t) -> p h t", h=NH)
                    for h in range(NH):
                        mm(psi[:, h, :], Sbf[s_][:, h, :], qb_sb[:, h, cs], True, True, [("Sbf", s_), "qb"], [("psmm", 3)])
                    P.op("dve", lambda e, cs=cs, psi=psi: e.tensor_tensor(out=o_sb[:, :, cs], in0=psi, in1=o_sb[:, :, cs], op=ALU.add),
                         r=[("psmm", 3)], w=["o"])
                    psU = psX[:, :].rearrange("p (h v) -> p h v", h=NH)
                    for h in range(NH):
                        hs = slice(h * 128, (h + 1) * 128)
                        mm(psU[:, h, :], kbT_sb[p0:p0 + 64, tb, hs], v_sb[p0:p0 + 64, tb, hs], True, True, ["kbT", "v"], [("psX", 0), ("psX", 1)])
                    P.op("dve", lambda e, s_=s_, psU=psU: e.tensor_tensor(out=Sst[s_], in0=psU, in1=Sst[s_], op=ALU.add),
                         r=[("psX", 0), ("psX", 1)], w=[("S", s_)])
                    yield
                for h in range(NH):
                    i = h % 2
                    pxk = ("psX", i)
                    pxa = psX[:, i * 512:i * 512 + T]
                    P.op("act", lambda e, h=h, i=i: e.activation(out=osq[i][:, 0:T], in_=o_sb[:, h, 0:T], func=AF.Square), r=["o"], w=[("osq", i)])
                    mm(pxa, ones[:, :], osq[i][:, 0:T], True, True, [("osq", i)], [pxk])
                    P.op("act", lambda e, i=i, pxa=pxa: e.activation(out=rst[i][:, 0:T], in_=pxa, func=AF.Ln, scale=1.0 / 128, bias=epsc[:, 0:1]),
                         r=[pxk], w=[("rst", i)])
                    P.op("act", lambda e, i=i: e.activation(out=rst[i][:, 0:T], in_=rst[i][:, 0:T], func=AF.Exp, scale=-0.5), r=[("rst", i)], w=[("rst", i)])
                    P.op("dve", lambda e, h=h, i=i: e.scalar_tensor_tensor(out=ytm[i][:, 0:T], in0=o_sb[:, h, 0:T], scalar=hng[:, l, h:h + 1],
                                                                          in1=rst[i][:, 0:T], op0=ALU.mult, op1=ALU.mult),
                         r=["o", ("rst", i)], w=[("ytm", i)])
                    P.op("pool", lambda e, h=h, i=i: e.tensor_tensor(out=mixT[:, h, 0:T], in0=ytm[i][:, 0:T], in1=gate_sb[:, h, 0:T], op=ALU.mult),
                         r=[("ytm", i), "gate"], w=[mk])
                    yield
                for pidx, (s, slot0, ntok, toff) in enumerate(tile["parts"]):
                    n = ntok + 16
                    for j in range(NH):
                        g = j // 2
                        w = POOLW[g]
                        x = upx[:, j, toff + 16 * pidx:toff + 16 * pidx + n]
                        eng = "pool" if j % 2 == 0 else "dve"
                        a1, a2, a3, a4 = [t[:, 0:n] for t in pa[eng]]
                        pk = [("pa", eng)]
                        P.op(eng, lambda e, x=x, a1=a1, n=n: e.tensor_tensor(out=a1[:, 1:n], in0=x[:, 0:n - 1], in1=x[:, 1:n], op=ALU.add), r=["upx"], w=pk)
                        cur = a1
                        if g >= 1:
                            P.op(eng, lambda e, a1=a1, a2=a2, n=n: e.tensor_tensor(out=a2[:, 2:n - 1], in0=a1[:, 1:n - 2], in1=a1[:, 3:n], op=ALU.add), r=pk, w=pk)
                            cur = a2
                        if g >= 2:
                            P.op(eng, lambda e, a2=a2, a3=a3, n=n: e.tensor_tensor(out=a3[:, 4:n - 3], in0=a2[:, 2:n - 5], in1=a2[:, 6:n - 1], op=ALU.add), r=pk, w=pk)
                            cur = a3
                        if g >= 3:
                            P.op(eng, lambda e, a3=a3, a4=a4, n=n: e.tensor_tensor(out=a4[:, 8:n - 7], in0=a3[:, 4:n - 11], in1=a3[:, 12:n - 3], op=ALU.add), r=pk, w=pk)
                            cur = a4
                        P.op("dve", lambda e, cur=cur, x=x, w=w, j=j, toff=toff, ntok=ntok: e.scalar_tensor_tensor(
                            out=pooled[:, j, toff:toff + ntok], in0=cur[:, 8:8 + ntok], scalar=1.0 / w, in1=x[:, 8:8 + ntok],
                            op0=ALU.mult, op1=ALU.subtract), r=pk + ["upx"], w=["pooled"])
                        fix = []
                        if tile["head"]:
                            for p in range(w // 2):
                                fix.append((48 + p, p + w - w // 2))
                        if tile["last"]:
                            for jj in range(1, w - w // 2):
                                fix.append((ntok - jj, w // 2 + jj))
                        for (col, cntv) in fix:
                            P.op("dve", lambda e, cur=cur, x=x, col=col, cntv=cntv, j=j, toff=toff: e.scalar_tensor_tensor(
                                out=pooled[:, j, toff + col:toff + col + 1], in0=cur[:, 8 + col:9 + col], scalar=1.0 / cntv,
                                in1=x[:, 8 + col:9 + col], op0=ALU.mult, op1=ALU.subtract), r=pk + ["upx"], w=["pooled"])
                        yield
                for g in range(4):
                    for ob in range(2):
                        i = ob
                        pxk = ("psX", i)
                        pxa = psX[:, i * 512:i * 512 + T]
                        for kc in range(2):
                            mm(pxa, wpool[:, l, g, kc, ob * 128:(ob + 1) * 128], pooled[:, 2 * g + kc, 0:T], kc == 0, kc == 1,
                               ["pooled"], [pxk])
                        P.op("act", lambda e, g=g, ob=ob, pxa=pxa: e.mul(out=mixT[:, 8 + 2 * g + ob, 0:T], in_=pxa,
                                                                         mul=psc[:, l, 2 * g + ob: 2 * g + ob + 1]),
                             r=[pxk], w=[mk])
                    yield

            def gen_F(tile, mixT, mk):
                T, nb = tile["T"], tile["nb"]
                if tile["head"]:
                    memset("pool", hT[:, 0, :], 0.0, [("hT", 0)])
                    for s in range(NS):
                        src = meta_d if l == 0 else H1[s][HEAD - NMETA:HEAD, :]
                        dma("sp", hT[64 * s + 48:64 * s + 64, 0, :], src, [("H", s)] if l else [], [("hT", 0)], ("hT", 0))
                else:
                    s, slot0 = tile["parts"][0][0], tile["parts"][0][1]
                    for tb in range(nb):
                        if l == 0:
                            src = x_d[s][slot0 - HEAD + tb * 128: slot0 - HEAD + (tb + 1) * 128, :]
                        else:
                            src = H1[s][slot0 + tb * 128: slot0 + (tb + 1) * 128, :]
                        dma("sp", hT[:, tb, :], src, [("H", s)] if l else [], [("hT", tb)], ("hT", tb))
                for pn in range(4):
                    slot = load_piece(WOUT[l][pn], ("WOUT", l, pn))

                    def oev(tb, ps, pk, pn=pn):
                        P.op("dve", lambda e: e.tensor_tensor(out=hT[:, tb, pn * 512:(pn + 1) * 512], in0=ps, in1=hT[:, tb, pn * 512:(pn + 1) * 512],
                                                              op=ALU.add), r=[pk], w=[("hT", tb)])
                    proj_tm(slot, mixT, mk, nb, oev, nbanks=3)
                    yield
                norm_transpose(tile, lambda tb: hT[:, tb, :], lambda tb: ("hT", tb), gml, mT, "mT", ab, ss)
                yield
                for fg in range(4):
                    hd = hid[fg % 2]
                    hk = ("hid", fg % 2)
                    for j in range(4):
                        slot = load_piece(WUP[l][fg * 4 + j], ("WUP", l, fg * 4 + j))

                        def uev(cb, ps, pk, j=j, hd=hd, hk=hk):
                            i = cb % 2
                            P.op("act", lambda e: e.activation(out=rl[i][:, 0:T], in_=ps, func=AF.Relu), r=[pk], w=[("rl", i)])
                            P.op("pool", lambda e: e.tensor_tensor(out=hd[:, j * 4 + cb, 0:T], in0=rl[i][:, 0:T], in1=rl[i][:, 0:T], op=ALU.mult),
                                 r=[("rl", i)], w=[hk])
                        proj_fm(slot, mT, "mT", T, uev, nbanks=3)
                        yield
                    for pn in range(4):
                        slot = load_piece(WDN[l][fg * 4 + pn], ("WDN", l, fg * 4 + pn))

                        def dev(tb, ps, pk, pn=pn):
                            P.op("dve", lambda e: e.tensor_tensor(out=hT[:, tb, pn * 512:(pn + 1) * 512], in0=ps, in1=hT[:, tb, pn * 512:(pn + 1) * 512],
                                                                  op=ALU.add), r=[pk], w=[("hT", tb)])
                        proj_tm(slot, hd, hk, nb, dev, nbanks=3)
                        yield
                if not last_layer:
                    if tile["head"]:
                        for s in range(NS):
                            dma("sp", H1[s][HEAD - NMETA:HEAD, :], hT[64 * s + 48:64 * s + 64, 0, :], [("hT", 0)], [("H", s)], ("H", "s"))
                    else:
                        s, slot0 = tile["parts"][0][0], tile["parts"][0][1]
                        for tb in range(nb):
                            dma("sp", H1[s][slot0 + tb * 128: slot0 + (tb + 1) * 128, :], hT[:, tb, :], [("hT", tb)], [("H", s)], ("H", "s", tb))
                else:
                    s, slot0 = tile["parts"][0][0], tile["parts"][0][1]
                    ykeys = [("ab", 0), ("ab", 1)]
                    for tb in range(nb):
                        hsrc = hT[:, tb, :]
                        P.op("act", lambda e, hsrc=hsrc, tb=tb: e.activation(out=yout, in_=hsrc, func=AF.Square, accum_out=ss[:, tb:tb + 1]),
                             r=[("hT", tb)], w=ykeys + [("ss", tb)])
                        P.op("act", lambda e, tb=tb: e.activation(out=ss[:, 8 + tb:9 + tb], in_=ss[:, tb:tb + 1], func=AF.Ln, scale=1.0 / D, bias=epsc[:, 0:1]),
                             r=[("ss", tb)], w=[("ss", tb)])
                        P.op("act", lambda e, tb=tb: e.activation(out=ss[:, 16 + tb:17 + tb], in_=ss[:, 8 + tb:9 + tb], func=AF.Exp, scale=-0.5),
                             r=[("ss", tb)], w=[("ss", tb)])
                        P.op("dve", lambda e, hsrc=hsrc, tb=tb: e.scalar_tensor_tensor(out=yout, in0=hsrc, scalar=ss[:, 16 + tb:17 + tb], in1=fgain,
                                                                                   op0=ALU.mult, op1=ALU.mult), r=[("hT", tb), ("ss", tb)], w=ykeys)
                        dma("sp", y_d[s][slot0 - HEAD + tb * 128: slot0 - HEAD + (tb + 1) * 128, :], yout, ykeys, [("Y", s)], ("Y", 0))
                yield

            order = [t for t in reversed(tiles) if not (t["head"] and last_layer)]
            gens_M = [gen_M(t, mixbuf[i % 2], ("mixT", i % 2)) for i, t in enumerate(order)]
            gens_F = [gen_F(t, mixbuf[i % 2], ("mixT", i % 2)) for i, t in enumerate(order)]
            for _ in gens_M[0]:
                pass
            for i in range(len(order)):
                f = gens_F[i]
                m = gens_M[i + 1] if i + 1 < len(order) else None
                fdone = False
                while not fdone or m is not None:
                    if not fdone:
                        try:
                            next(f)
                        except StopIteration:
                            fdone = True
                    if m is not None:
                        try:
                            next(m)
                        except StopIteration:
                            m = None
            P.barrier()

        for l in range(debug_layers):
            run_layer(l, l == debug_layers - 1)
        P.barrier()

        P.finalize(nc, stack)
        DEBUG['_P'] = P
        with nc.Block() as block:
            P.replay(nc, block)
    return nc


_CACHE = {}


def kernel(x_prompt, x_sample, meta_tokens, w_in, w_pool, pool_scale, hg_lower_bound, hg_head_norm, w_out,
           norm_mix, norm_mlp, w_up, w_down, final_norm):
    x_prompt = np.asarray(x_prompt, dtype=np.float32)
    x_sample = np.asarray(x_sample, dtype=np.float32)
    SL, SS = x_prompt.shape[1], x_sample.shape[1]
    nprompt, nsample = x_prompt.shape[0], x_sample.shape[0]
    key = (SL, SS)
    if key not in _CACHE:
        _CACHE[key] = build_program([SL, SS])
    nc = _CACHE[key]
    shared = dict(meta_tokens=meta_tokens, w_in=w_in, w_pool=w_pool, pool_scale=pool_scale, hg_lower_bound=hg_lower_bound,
                  hg_head_norm=hg_head_norm, w_out=w_out, norm_mix=norm_mix, norm_mlp=norm_mlp, w_up=w_up, w_down=w_down,
                  final_norm=final_norm)
    shared = {k: np.ascontiguousarray(np.asarray(v, dtype=np.float32)) for k, v in shared.items()}
    zeros_long = np.zeros((SL, D), np.float32)
    in_maps = []
    for c in range(8):
        m = dict(shared)
        m["x0"] = np.ascontiguousarray(x_prompt[c]) if c < nprompt else zeros_long
        m["x1"] = np.ascontiguousarray(x_sample[c % nsample])
        in_maps.append(m)
    res = run_bass_kernel_spmd(nc, in_maps, core_ids=list(range(8)))
    y_prompt = np.stack([res.results[c]["y0"] for c in range(nprompt)], axis=0)
    y_sample = np.stack([res.results[c]["y1"] for c in range(nsample)], axis=0)
    return (y_prompt.astype(np.float32), y_sample.astype(np.float32))
```
